# Optimizing a Trainium2 kernel written in Bass

```python
import math
import jax, jax.numpy as jnp
from jax import lax
import numpy as np

D_MODEL = 1024
BATCH = 16
SEQ = 4096
DEPTH = 4

GRID_W = 64
CTX_LEN = 256
N_MIXERS = 3
D_FF = 4 * D_MODEL
RMS_EPS = 1e-6
ROPE_BASE = 10000.0
Q_BLOCK = 128
FOURIER_GROUPS = 4
DIFF_HEADS = D_MODEL // 128
DIFF_HEAD_DIM = D_MODEL // (2 * DIFF_HEADS)
DIFF_V_DIM = 2 * DIFF_HEAD_DIM
MLA_HEADS = 16
MLA_NOPE_DIM = 64
MLA_ROPE_DIM = 32
MLA_V_DIM = 64
MLA_Q_RANK = 256
MLA_KV_RANK = 128
MLA_DOWN = MLA_Q_RANK + MLA_KV_RANK + MLA_ROPE_DIM
N_FOURIER = (DEPTH + 2) // 3
N_DIFF = (DEPTH + 1) // 3
N_MLA = DEPTH // 3

kernel_name = "hybrid_fourier_diffattn_mla_dit"


def rms_norm(x, g, eps=RMS_EPS):
    x32 = x.astype(jnp.float32)
    y = x32 * lax.rsqrt(jnp.mean(x32 * x32, axis=-1, keepdims=True) + eps)
    return y.astype(x.dtype) * g


def modulate(x, g, shift, scale):
    return rms_norm(x, g) * (1 + scale) + shift


def axial_rope_tables(rows, dim):
    n_freq = dim // 4
    inv_freq = ROPE_BASE ** (-jnp.arange(n_freq, dtype=jnp.float32) / n_freq)
    t = jnp.arange(rows * GRID_W)
    row = (t // GRID_W).astype(jnp.float32)
    col = (t % GRID_W).astype(jnp.float32)
    ang = jnp.stack([row[:, None] * inv_freq, col[:, None] * inv_freq], axis=1)
    return jnp.cos(ang), jnp.sin(ang)


def apply_axial_rope(x, cos, sin):
    shp = x.shape
    n_freq = shp[-1] // 4
    xs = x.reshape(shp[:-1] + (2, 2, n_freq))
    x1, x2 = xs[..., 0, :], xs[..., 1, :]
    cb = cos[None, :, None].astype(x.dtype)
    sb = sin[None, :, None].astype(x.dtype)
    out = jnp.stack([x1 * cb - x2 * sb, x2 * cb + x1 * sb], axis=-2)
    return out.reshape(shp)


def sweep_query_blocks(fn, *qs):
    b, s = qs[0].shape[:2]
    nb = s // Q_BLOCK
    blocks = tuple(jnp.moveaxis(q.reshape((b, nb, Q_BLOCK) + q.shape[2:]), 1, 0) for q in qs)
    out = lax.map(lambda blk: fn(*blk), blocks)
    return jnp.moveaxis(out, 0, 1).reshape((b, s) + out.shape[3:])


def sq_relu_mlp(h, w1, w2):
    return jnp.square(jax.nn.relu(h @ w1)) @ w2


def fourier_mixer(h, w, bias):
    b, s, d = h.shape
    hg = h.astype(jnp.float32).reshape(b, s, FOURIER_GROUPS, d // FOURIER_GROUPS)
    f = jnp.fft.fft2(hg, axes=(1, 3), norm="ortho").real
    return f.reshape(b, s, d).astype(h.dtype) @ w + bias


def diff_project(h, w_qkv, rope):
    b, s, _ = h.shape
    q, k, v = jnp.split(h @ w_qkv, 3, axis=-1)
    q = q.reshape(b, s, 2 * DIFF_HEADS, DIFF_HEAD_DIM)
    k = k.reshape(b, s, 2 * DIFF_HEADS, DIFF_HEAD_DIM)
    if rope is not None:
        q = apply_axial_rope(q, *rope)
        k = apply_axial_rope(k, *rope)
    q = q.reshape(b, s, DIFF_HEADS, 2, DIFF_HEAD_DIM)
    k = k.reshape(b, s, DIFF_HEADS, 2, DIFF_HEAD_DIM)
    v = v.reshape(b, s, DIFF_HEADS, DIFF_V_DIM)
    return q[..., 0, :], q[..., 1, :], k[..., 0, :], k[..., 1, :], v


def diff_attend(q1, q2, k1, k2, v, lam):
    scale = DIFF_HEAD_DIM ** -0.5
    s1 = jnp.einsum("bqhd,bkhd->bhqk", q1, k1).astype(jnp.float32) * scale
    s2 = jnp.einsum("bqhd,bkhd->bhqk", q2, k2).astype(jnp.float32) * scale
    p = jax.nn.softmax(s1, axis=-1) - lam * jax.nn.softmax(s2, axis=-1)
    return jnp.einsum("bhqk,bkhe->bqhe", p.astype(v.dtype), v)


def diff_attention_mixer(h_lat, h_ctx, w_qkv, lq1, lk1, lq2, lk2, subln_g, w_o, layer_idx, rope, with_ctx):
    lam_init = 0.8 - 0.6 * math.exp(-0.3 * layer_idx)
    lam = (jnp.exp(jnp.sum(lq1.astype(jnp.float32) * lk1.astype(jnp.float32)))
           - jnp.exp(jnp.sum(lq2.astype(jnp.float32) * lk2.astype(jnp.float32))) + lam_init)
    q1l, q2l, k1l, k2l, vl = diff_project(h_lat, w_qkv, rope)
    q1c, q2c, k1c, k2c, vc = diff_project(h_ctx, w_qkv, None)
    k1a = jnp.concatenate([k1c, k1l], axis=1)
    k2a = jnp.concatenate([k2c, k2l], axis=1)
    va = jnp.concatenate([vc, vl], axis=1)

    def finish(o):
        b, s = o.shape[:2]
        return (rms_norm(o, subln_g) * (1 - lam_init)).reshape(b, s, D_MODEL) @ w_o

    o_lat = sweep_query_blocks(lambda a, bq: diff_attend(a, bq, k1a, k2a, va, lam), q1l, q2l)
    y_lat = finish(o_lat)
    y_ctx = finish(diff_attend(q1c, q2c, k1c, k2c, vc, lam)) if with_ctx else None
    return y_lat, y_ctx


def mla_project(h, w_down, q_norm_g, kv_norm_g, w_uq, w_ukv, rope):
    b, s, _ = h.shape
    cq, ckv, k_rope = jnp.split(h @ w_down, [MLA_Q_RANK, MLA_Q_RANK + MLA_KV_RANK], axis=-1)
    q = (rms_norm(cq, q_norm_g) @ w_uq).reshape(b, s, MLA_HEADS, MLA_NOPE_DIM + MLA_ROPE_DIM)
    q_nope, q_rope = jnp.split(q, [MLA_NOPE_DIM], axis=-1)
    kv = (rms_norm(ckv, kv_norm_g) @ w_ukv).reshape(b, s, MLA_HEADS, MLA_NOPE_DIM + MLA_V_DIM)
    k_nope, v = jnp.split(kv, [MLA_NOPE_DIM], axis=-1)
    k_rope = k_rope[:, :, None, :]
    if rope is not None:
        q_rope = apply_axial_rope(q_rope, *rope)
        k_rope = apply_axial_rope(k_rope, *rope)
    return q_nope, q_rope, k_nope, k_rope[:, :, 0], v


def mla_attend(q_nope, q_rope, k_nope, k_rope, v):
    scale = (MLA_NOPE_DIM + MLA_ROPE_DIM) ** -0.5
    s = (jnp.einsum("bqhd,bkhd->bhqk", q_nope, k_nope)
         + jnp.einsum("bqhr,bkr->bhqk", q_rope, k_rope)).astype(jnp.float32) * scale
    p = jax.nn.softmax(s, axis=-1)
    return jnp.einsum("bhqk,bkhe->bqhe", p.astype(v.dtype), v)


def mla_mixer(h_lat, h_ctx, w_down, q_norm_g, kv_norm_g, w_uq, w_ukv, w_o, rope, with_ctx):
    qn_l, qr_l, kn_l, kr_l, v_l = mla_project(h_lat, w_down, q_norm_g, kv_norm_g, w_uq, w_ukv, rope)
    qn_c, qr_c, kn_c, kr_c, v_c = mla_project(h_ctx, w_down, q_norm_g, kv_norm_g, w_uq, w_ukv, None)
    kn_a = jnp.concatenate([kn_c, kn_l], axis=1)
    kr_a = jnp.concatenate([kr_c, kr_l], axis=1)
    v_a = jnp.concatenate([v_c, v_l], axis=1)
    o_lat = sweep_query_blocks(lambda qn, qr: mla_attend(qn, qr, kn_a, kr_a, v_a), qn_l, qr_l)
    b, s = o_lat.shape[:2]
    y_lat = o_lat.reshape(b, s, MLA_HEADS * MLA_V_DIM) @ w_o
    y_ctx = None
    if with_ctx:
        o_ctx = mla_attend(qn_c, qr_c, kn_c, kr_c, v_c)
        y_ctx = o_ctx.reshape(o_ctx.shape[0], o_ctx.shape[1], MLA_HEADS * MLA_V_DIM) @ w_o
    return y_lat, y_ctx


def setup_inputs(seed: int = 0) -> dict:
    key = jax.random.key(seed)
    ks = jax.random.split(key, 26)

    def nrm(k, shape, scale):
        return jax.random.normal(k, shape, jnp.float32) * scale

    def gain(k, shape):
        return 1.0 + 0.05 * jax.random.normal(k, shape, jnp.float32)

    d = D_MODEL
    return {
        "x": nrm(ks[0], (BATCH, SEQ, d), 1.0),
        "c": nrm(ks[1], (BATCH, d), 1.0),
        "ctx": nrm(ks[2], (BATCH, CTX_LEN, d), 1.0),
        "c_ctx": nrm(ks[3], (d,), 1.0),
        "mod_w": nrm(ks[4], (DEPTH, d, 6 * d), 0.5 * d ** -0.5),
        "mod_b": nrm(ks[5], (DEPTH, 6 * d), 0.02),
        "norm_mix_g": gain(ks[6], (DEPTH, d)),
        "norm_mlp_g": gain(ks[7], (DEPTH, d)),
        "final_g": gain(ks[8], (d,)),
        "mlp_w1": nrm(ks[9], (DEPTH, d, D_FF), d ** -0.5),
        "mlp_w2": nrm(ks[10], (DEPTH, D_FF, d), D_FF ** -0.5),
        "fourier_w": nrm(ks[11], (N_FOURIER, d, d), d ** -0.5),
        "fourier_b": nrm(ks[12], (N_FOURIER, d), 0.02),
        "diff_w_qkv": nrm(ks[13], (N_DIFF, d, 3 * d), d ** -0.5),
        "diff_lambda_q1": nrm(ks[14], (N_DIFF, DIFF_HEAD_DIM), 0.1),
        "diff_lambda_k1": nrm(ks[15], (N_DIFF, DIFF_HEAD_DIM), 0.1),
        "diff_lambda_q2": nrm(ks[16], (N_DIFF, DIFF_HEAD_DIM), 0.1),
        "diff_lambda_k2": nrm(ks[17], (N_DIFF, DIFF_HEAD_DIM), 0.1),
        "diff_subln_g": gain(ks[18], (N_DIFF, DIFF_V_DIM)),
        "diff_w_o": nrm(ks[19], (N_DIFF, d, d), d ** -0.5),
        "mla_w_down": nrm(ks[20], (N_MLA, d, MLA_DOWN), d ** -0.5),
        "mla_q_norm_g": gain(ks[21], (N_MLA, MLA_Q_RANK)),
        "mla_kv_norm_g": gain(ks[22], (N_MLA, MLA_KV_RANK)),
        "mla_w_uq": nrm(ks[23], (N_MLA, MLA_Q_RANK, MLA_HEADS * (MLA_NOPE_DIM + MLA_ROPE_DIM)), MLA_Q_RANK ** -0.5),
        "mla_w_ukv": nrm(ks[24], (N_MLA, MLA_KV_RANK, MLA_HEADS * (MLA_NOPE_DIM + MLA_V_DIM)), MLA_KV_RANK ** -0.5),
        "mla_w_o": nrm(ks[25], (N_MLA, MLA_HEADS * MLA_V_DIM, d), (MLA_HEADS * MLA_V_DIM) ** -0.5),
    }


def reference(x, c, ctx, c_ctx, mod_w, mod_b, norm_mix_g, norm_mlp_g, final_g, mlp_w1, mlp_w2,
              fourier_w, fourier_b, diff_w_qkv, diff_lambda_q1, diff_lambda_k1, diff_lambda_q2,
              diff_lambda_k2, diff_subln_g, diff_w_o, mla_w_down, mla_q_norm_g, mla_kv_norm_g,
              mla_w_uq, mla_w_ukv, mla_w_o):
    rows = x.shape[1] // GRID_W
    rope_diff = axial_rope_tables(rows, DIFF_HEAD_DIM)
    rope_mla = axial_rope_tables(rows, MLA_ROPE_DIM)
    silu_c = jax.nn.silu(c)[:, None, :]
    silu_cc = jax.nn.silu(c_ctx)
    x_lat, x_ctx = x, ctx
    for i in range(DEPTH):
        kind, j = i % N_MIXERS, i // N_MIXERS
        update_ctx = i < DEPTH - 1
        ctx_feeds_mixer = update_ctx or kind != 0
        sh_a, sc_a, g_a, sh_m, sc_m, g_m = jnp.split(silu_c @ mod_w[i] + mod_b[i], 6, axis=-1)
        h_lat = modulate(x_lat, norm_mix_g[i], sh_a, sc_a)
        if ctx_feeds_mixer:
            csh_a, csc_a, cg_a, csh_m, csc_m, cg_m = jnp.split(silu_cc @ mod_w[i] + mod_b[i], 6, axis=-1)
            h_ctx = modulate(x_ctx, norm_mix_g[i], csh_a, csc_a)
        if kind == 0:
            y_lat = fourier_mixer(h_lat, fourier_w[j], fourier_b[j])
            y_ctx = fourier_mixer(h_ctx, fourier_w[j], fourier_b[j]) if update_ctx else None
        elif kind == 1:
            y_lat, y_ctx = diff_attention_mixer(
                h_lat, h_ctx, diff_w_qkv[j], diff_lambda_q1[j], diff_lambda_k1[j], diff_lambda_q2[j],
                diff_lambda_k2[j], diff_subln_g[j], diff_w_o[j], i, rope_diff, update_ctx)
        else:
            y_lat, y_ctx = mla_mixer(
                h_lat, h_ctx, mla_w_down[j], mla_q_norm_g[j], mla_kv_norm_g[j], mla_w_uq[j],
                mla_w_ukv[j], mla_w_o[j], rope_mla, update_ctx)
        x_lat = x_lat + g_a * y_lat
        x_lat = x_lat + g_m * sq_relu_mlp(modulate(x_lat, norm_mlp_g[i], sh_m, sc_m), mlp_w1[i], mlp_w2[i])
        if update_ctx:
            x_ctx = x_ctx + cg_a * y_ctx
            x_ctx = x_ctx + cg_m * sq_relu_mlp(modulate(x_ctx, norm_mlp_g[i], csh_m, csc_m), mlp_w1[i], mlp_w2[i])
    return rms_norm(x_lat, final_g)
```

```python
import math
import numpy as np
import ml_dtypes
import concourse.bass as bass
import concourse.mybir as mybir
from concourse.bass_utils import run_bass_kernel_spmd

F32 = mybir.dt.float32
BF16 = mybir.dt.bfloat16
ALU = mybir.AluOpType
AF = mybir.ActivationFunctionType
AX = mybir.AxisListType

D = 1024
S = 4096
C = 256
NB = 2
TOK = NB * (S + C)
T = 256
NT = TOK // T
DEPTH = 4
EPS = 1e-6
SAME_ENG_SYNC = True


def tinfo(t):
    if t < 2:
        return dict(ctx=True, lb=t, pos=0, cond=2, tok0=t * T)
    lb = (t - 2) // 16
    j = (t - 2) % 16
    return dict(ctx=False, lb=lb, pos=j * T, cond=lb, tok0=t * T)


def ctx_tok(lb):
    return lb * C


def lat_tok(lb):
    return 2 * C + lb * S


class Buf:
    __slots__ = ("w", "r", "name")

    def __init__(self, name=""):
        self.w = {}
        self.r = {}
        self.name = name


class KB:
    def __init__(self, nc):
        self.nc = nc
        self.sems = []
        self.E = {"pe": nc.tensor, "act": nc.scalar, "dve": nc.vector, "pool": nc.gpsimd, "sp": nc.sync}
        self.csem = {}
        self.ccnt = {}
        for e in ("pe", "act", "dve", "pool"):
            self.csem[e] = self.new_sem("c_" + e)
            self.ccnt[e] = 0
        self.rings = {}
        for q, n in (("sp", 28), ("pool", 16), ("act", 4)):
            self.rings[q] = dict(sems=[self.new_sem("d_%s%d" % (q, i)) for i in range(n)], cnt=[0] * n, nxt=0)
        self.waited = {e: {} for e in self.E}
        self.pend = {e: [] for e in self.csem}
        self.n_ins = 0

    def new_sem(self, name):
        h = self.nc.semaphore(name).__enter__()
        self.sems.append(h)
        return len(self.sems) - 1

    @staticmethod
    def _merge(d, s):
        for k, v in s.items():
            if v > d.get(k, 0):
                d[k] = v

    @staticmethod
    def _flat(bs):
        out = []
        for b in bs:
            if isinstance(b, (list, tuple)):
                out.extend(KB._flat(b))
            else:
                out.append(b)
        return out

    def _deps(self, reads, writes):
        reads = self._flat(reads)
        writes = self._flat(writes)
        d = {}
        for b in reads:
            self._merge(d, b.w)
        for b in writes:
            self._merge(d, b.w)
            self._merge(d, b.r)
        return d

    def _wait(self, e, d):
        own = self.csem.get(e)
        wd = self.waited[e]
        for s, v in d.items():
            if v <= wd.get(s, 0):
                continue
            if s == own and (e == "pe" or not SAME_ENG_SYNC):
                continue
            self.E[e].wait_ge(self.sems[s], v)
            self.n_ins += 1
            wd[s] = v

    def _finish(self, tok, reads, writes):
        reads = self._flat(reads)
        writes = self._flat(writes)
        s, v = tok
        for b in reads:
            if v > b.r.get(s, 0):
                b.r[s] = v
        for b in writes:
            if v > b.w.get(s, 0):
                b.w[s] = v

    def op(self, e, fn, reads=(), writes=(), inc=True):
        self._wait(e, self._deps(reads, writes))
        ins = fn(self.E[e])
        self.n_ins += 1
        self.pend[e].append((reads, writes))
        if inc:
            self.ccnt[e] += 1
            ins.then_inc(self.sems[self.csem[e]], 1)
            tok = (self.csem[e], self.ccnt[e])
            for r, w in self.pend[e]:
                self._finish(tok, r, w)
            self.pend[e] = []

    def dma(self, q, out, in_, reads=(), writes=()):
        R = self.rings[q]
        j = R["nxt"]
        R["nxt"] = (j + 1) % len(R["sems"])
        s = R["sems"][j]
        d = self._deps(reads, writes)
        if R["cnt"][j] > d.get(s, 0):
            d[s] = R["cnt"][j]
        self._wait(q, d)
        R["cnt"][j] += 16
        self.E[q].dma_start(out=out, in_=in_).then_inc(self.sems[s], 16)
        self.n_ins += 1
        self._finish((s, R["cnt"][j]), reads, writes)

    def barrier(self):
        for e in self.pend:
            assert not self.pend[e], "pending un-inc'd ops on " + e
        d = {}
        for e in self.csem:
            if self.ccnt[e] > 0:
                d[self.csem[e]] = self.ccnt[e]
        for R in self.rings.values():
            for s, c in zip(R["sems"], R["cnt"]):
                if c:
                    d[s] = c
        for e in self.E:
            self._wait(e, dict(d))

    def mm(self, out_ap, out_buf, pairs, reads):
        n = len(pairs)
        for i, (l, r) in enumerate(pairs):
            self.op("pe",
                    lambda e, l=l, r=r, i=i: e.matmul(out_ap, lhsT=l, rhs=r, start=(i == 0), stop=(i == n - 1)),
                    reads=reads if i == 0 else (), writes=[out_buf] if i == 0 else (), inc=(i == n - 1))


class Rot:
    def __init__(self, aps, name, bufs=None):
        self.aps = aps
        self.bufs = bufs if bufs is not None else [Buf("%s%d" % (name, i)) for i in range(len(aps))]
        self.i = 0

    def next(self):
        j = self.i
        self.i = (j + 1) % len(self.aps)
        return self.aps[j], self.bufs[j]


def _bf(a):
    return np.ascontiguousarray(np.asarray(a, dtype=np.float32).astype(ml_dtypes.bfloat16))


def make_consts():
    cst = {}
    cst["ident"] = np.eye(128, dtype=np.float32)
    t = np.arange(S)
    row = (t // 64).astype(np.float64)
    col = (t % 64).astype(np.float64)

    def rope_tab(dim, dvals):
        nf = dim // 4
        inv = 10000.0 ** (-np.arange(nf, dtype=np.float64) / nf)
        cos = np.zeros((len(dvals), S))
        sin = np.zeros((len(dvals), S))
        for i, d in enumerate(dvals):
            axis = d // (dim // 2)
            pair = (d % (dim // 2)) // nf
            f = d % nf
            ang = (row if axis == 0 else col) * inv[f]
            ang = ang.astype(np.float32).astype(np.float64)
            cos[i] = np.cos(ang)
            sin[i] = np.sin(ang) * (-1.0 if pair == 0 else 1.0)
        return cos, sin

    c64, s64 = rope_tab(64, list(range(64)))
    cst["cosD"] = np.concatenate([c64, c64], 0).astype(np.float32)
    cst["sinD"] = np.concatenate([s64, s64], 0).astype(np.float32)
    P = np.zeros((128, 128), np.float32)
    for m in range(128):
        P[(m // 64) * 64 + ((m % 64) ^ 16), m] = 1.0
    cst["permD"] = _bf(P)
    c32, s32 = rope_tab(32, list(range(32)))
    cst["cosM"] = np.concatenate([np.ones((64, S)), c32], 0).astype(np.float32)
    cst["sinM"] = np.concatenate([np.zeros((64, S)), s32], 0).astype(np.float32)
    cst["cosK"] = c32.astype(np.float32)
    cst["sinK"] = s32.astype(np.float32)
    P = np.zeros((96, 96), np.float32)
    for m in range(64, 96):
        P[64 + ((m - 64) ^ 8), m] = 1.0
    cst["permM"] = _bf(P)
    P = np.zeros((32, 32), np.float32)
    for m in range(32):
        P[m ^ 8, m] = 1.0
    cst["permK"] = _bf(P)
    c = np.arange(256)
    ang = 2 * np.pi * np.outer(c, c) / 256.0
    cst["C256r"] = np.ascontiguousarray(np.cos(ang).reshape(2, 128, 256).transpose(1, 0, 2)).astype(np.float32)
    cst["S256r"] = np.ascontiguousarray(np.sin(ang).reshape(2, 128, 256).transpose(1, 0, 2)).astype(np.float32)
    cst["C256s"] = _bf((np.cos(ang) / 256.0).reshape(2, 128, 256).transpose(1, 0, 2))
    cst["nS256s"] = _bf((-np.sin(ang) / 256.0).reshape(2, 128, 256).transpose(1, 0, 2))
    TA = np.zeros((128, 32, 3, 128), np.float64)
    bb = np.arange(32)
    for q in range(32):
        for al in range(4):
            a = 4 * q + al
            phi = 2 * np.pi * (np.outer(bb, bb) / 32.0 + a * bb[None, :] / 4096.0)
            sl = slice(32 * al, 32 * al + 32)
            TA[sl, q, 0, sl] = np.cos(phi)
            TA[sl, q, 1, sl] = np.sin(phi)
            TA[sl, q, 2, sl] = -np.sin(phi)
    cst["TA"] = _bf(TA)
    a = np.arange(128)
    psi = 2 * np.pi * np.outer(a, a) / 128.0
    TB = np.stack([np.cos(psi) / 1024.0, -np.sin(psi) / 1024.0], 1)
    cst["TB"] = _bf(TB)
    return cst


class Prog:
    def __init__(self, n_layers=DEPTH, debug=False, max_phase=99, noinput=False):
        self.noinput = noinput
        self.max_phase = max_phase
        self.n_layers = n_layers
        self.debug = debug
        nc = bass.Bass("TRN2", target_bir_lowering=False)
        self.nc = nc
        self.kb = KB(nc)
        self.din = {}
        self.stack = []

    def ph(self, fn, *a):
        self._nph += 1
        if self._nph <= self.max_phase:
            fn(*a)

    def inp(self, name, shape, dt=F32):
        t = self.nc.dram_tensor(name, list(shape), dt, kind="Internal" if self.noinput else "ExternalInput").ap()
        self.din[name] = t
        return t

    def scratch(self, name, shape, dt, out=False):
        return self.nc.dram_tensor(name, list(shape), dt, kind="ExternalOutput" if (out and not self.noinput) else "Internal").ap()

    def sb(self, name, shape, dt):
        self._uid = getattr(self, "_uid", 0) + 1
        cm = self.nc.sbuf_tensor("s%d_%s" % (self._uid, name), list(shape), dt)
        h = cm.__enter__()
        self.stack.append(cm)
        return h

    def mark(self):
        return len(self.stack)

    def release(self, mark):
        while len(self.stack) > mark:
            self.stack.pop().__exit__(None, None, None)

    def build(self):
        nc, kb = self.nc, self.kb
        L = self.n_layers
        x_in = self.inp("x", [NB, S, D])
        ctx_in = self.inp("ctx", [NB, C, D])
        condT = self.inp("condT", [128, 8, 3])
        mod_w = self.inp("mod_w", [DEPTH, D, 6 * D])
        mod_bT = self.inp("mod_bT", [128, DEPTH, 48])
        gmixT = self.inp("gmixT", [128, DEPTH, 8])
        gmlpT = self.inp("gmlpT", [128, DEPTH, 8])
        gfinT = self.inp("gfinT", [128, 8])
        mlp_w1 = self.inp("mlp_w1", [DEPTH, D, 4 * D])
        mlp_w2 = self.inp("mlp_w2", [DEPTH, 4 * D, D])
        fw = self.inp("fourier_w", [2, D, D])
        fbT = self.inp("fbT", [128, 2, 8])
        dqkv = self.inp("diff_w_qkv", [1, D, 3 * D])
        dlam = self.inp("dlam", [128, 4, 64])
        dsub = self.inp("dsub", [128, 1])
        dwo = self.inp("diff_w_o", [1, D, D])
        mdown = self.inp("mla_w_down", [1, D, 416])
        mgq = self.inp("mgq", [128, 2])
        mgkv = self.inp("mgkv", [128, 1])
        muq = self.inp("mla_w_uq", [1, 256, 1536])
        mukv = self.inp("mla_w_ukv", [1, 128, 2048])
        mwo = self.inp("mla_w_o", [1, D, D])
        cs = {}
        for k, (shape, dt) in dict(
            ident=([128, 128], F32), cosD=([128, S], F32), sinD=([128, S], F32), permD=([128, 128], BF16),
            cosM=([96, S], F32), sinM=([96, S], F32), cosK=([32, S], F32), sinK=([32, S], F32),
            permM=([96, 96], BF16), permK=([32, 32], BF16), C256r=([128, 2, 256], F32), S256r=([128, 2, 256], F32),
            C256s=([128, 2, 256], BF16), nS256s=([128, 2, 256], BF16), TA=([128, 32, 3, 128], BF16),
            TB=([128, 2, 128], BF16)).items():
            cs[k] = self.inp(k, shape, dt)
        self.cs = cs
        out = self.nc.dram_tensor("out", [NB, S, D], F32, kind="ExternalOutput").ap()
        dbg = self.debug
        self.XT = self.scratch("XT", [8, 128, TOK], F32, out=dbg)
        self.XTb = [Buf("XT%d" % t) for t in range(NT)]
        self.MT = self.scratch("MT", [8, 128, TOK], BF16, out=dbg)
        self.MTb = [Buf("MT%d" % t) for t in range(NT)]
        self.ZS = self.scratch("ZS", [TOK, 2048], BF16)
        self.ZSb = [Buf() for t in range(NT)]
        self.VSF = self.scratch("VSF", [NB, S, 2048], BF16)
        self.VSFb = [Buf() for _ in range(NB)]
        self.QK = self.scratch("QK", [16, 128, TOK], BF16)
        self.QKb = [Buf() for t in range(NT)]
        self.VS = self.scratch("VS", [TOK, 1024], BF16)
        self.VSb = [Buf() for t in range(NT)]
        self.KT = self.scratch("KT", [16, 64, TOK], BF16)
        self.KTb = [Buf() for t in range(NT)]
        self.KR = self.scratch("KR", [32, TOK], BF16)
        self.KRb = [Buf() for t in range(NT)]
        self.QM = self.scratch("QM", [16, 96, TOK], BF16)
        self.QMb = [Buf() for t in range(NT)]
        self.VA = self.scratch("VA", [TOK, 2048], BF16)
        self.VAb = [Buf() for t in range(NT)]
        if dbg:
            self.XS = self.scratch("XS", [DEPTH, 8, 128, 4 * T], F32, out=True)
            self.XSb = Buf("XS")
        self.x_in, self.ctx_in, self.out = x_in, ctx_in, out
        self.outb = Buf("out")

        self.ps = nc.psum_tensor("ps", [128, 4096], F32).__enter__()
        self.psh = [Buf("psb%d" % i) for i in range(8)]

        self.cb = Buf("consts")
        self.ones = self.sb("ones", [128, 128], BF16)
        kb.op("dve", lambda e: e.memset(self.ones[:], 1.0), writes=[self.cb])
        self.ident = self.sb("ident", [128, 128], F32)
        kb.dma("sp", self.ident[:], cs["ident"], writes=[self.cb])
        self.modT = self.sb("modT", [128, DEPTH, 48, 3], F32)
        self.gsA = self.sb("gsA", [128, DEPTH, 8, 3], F32)
        self.gsM = self.sb("gsM", [128, DEPTH, 8, 3], F32)
        self.gb = self.sb("gb", [128, DEPTH, 8, 3], F32)
        self.gfin = self.sb("gfin", [128, 8], F32)
        kb.dma("sp", self.gfin[:], gfinT, writes=[self.cb])
        self.modb = Buf("mod")

        self.phase_mod(condT, mod_w, mod_bT, gmixT, gmlpT, fbT)
        kb.barrier()
        if self.debug:
            dm = self.scratch("dbgmod", [128, L * 48 * 3], F32, out=True)
            kb.dma("sp", dm, self.modT[:, 0:L].rearrange("p l o s -> p (l o s)"), reads=[self.modb])
            kb.barrier()
        self._nph = 0
        for l in range(L):
            kind = l % 3
            last = (l == DEPTH - 1)
            if kind == 0:
                self.ph(self.fourier_front, l, fw[l // 3])
                kb.barrier()
                self.ph(self.fourier_mid, l)
                kb.barrier()
                self.ph(self.tail, l, None, last)
            elif kind == 1:
                self.ph(self.diff_front, l, dqkv[0])
                kb.barrier()
                self.ph(self.diff_mid, l, dlam, dsub)
                kb.barrier()
                self.ph(self.tail, l, dwo[0], last)
            else:
                self.ph(self.mla_front, l, mdown[0], mgq, mgkv, muq[0], mukv[0])
                kb.barrier()
                self.ph(self.mla_mid, l)
                kb.barrier()
                self.ph(self.tail, l, mwo[0], last)
            kb.barrier()
        kb.barrier()
        return nc

    def phase_mod(self, condT, mod_w, mod_bT, gmixT, gmlpT, fbT):
        kb = self.kb
        mk = self.mark()
        cond = self.sb("cond", [128, 8, 3], F32)
        silu = self.sb("silu", [128, 8, 3], F32)
        mb = self.sb("mb", [128, DEPTH, 48], F32)
        gmix = self.sb("gmix", [128, DEPTH, 8], F32)
        gmlp = self.sb("gmlp", [128, DEPTH, 8], F32)
        fb = self.sb("fb", [128, 2, 8], F32)
        b0 = Buf("m0")
        kb.dma("sp", cond[:], condT, writes=[b0])
        kb.dma("sp", mb[:], mod_bT, writes=[b0])
        kb.dma("sp", gmix[:], gmixT, writes=[b0])
        kb.dma("sp", gmlp[:], gmlpT, writes=[b0])
        kb.dma("sp", fb[:], fbT, writes=[b0])
        bs = Buf("silu")
        kb.op("act", lambda e: e.activation(out=silu[:], in_=cond[:], func=AF.Silu), reads=[b0], writes=[bs])
        wst = [self.sb("mwst%d" % i, [128, 8, 1024], F32) for i in range(2)]
        wrot = Rot([w[:] for w in wst], "mwst")
        psb = self.preg(0, 192)[1]
        for l in range(self.n_layers):
            for g in range(6):
                wap, wbuf = wrot.next()
                src = mod_w[l].rearrange("(kc p) n -> p kc n", p=128)[:, :, g * 1024:(g + 1) * 1024]
                kb.dma("sp" if g % 2 == 0 else "pool", wap, src, writes=[wbuf])
                for o in range(8):
                    oc = g * 8 + o
                    pairs = [(wap[:, k, o * 128:(o + 1) * 128], silu[:, k, :]) for k in range(8)]
                    kb.mm(self.ps[:, oc * 4:oc * 4 + 3], psb, pairs, reads=[wbuf, bs])
            psv = self.ps[:, 0:192].rearrange("p (o s) -> p o s", s=4)
            for s_ in range(3):
                kb.op("dve", lambda e, s_=s_, l=l: e.tensor_tensor(out=self.modT[:, l, :, s_], in0=psv[:, :, s_],
                                                                   in1=mb[:, l, :], op=ALU.add),
                      reads=[psb, b0], writes=[self.modb])
            for s_ in range(3):
                kb.op("dve", lambda e, s_=s_, l=l: e.scalar_tensor_tensor(
                    out=self.gsA[:, l, :, s_], in0=self.modT[:, l, 8:16, s_], scalar=1.0, in1=gmix[:, l, :],
                    op0=ALU.add, op1=ALU.mult), reads=[self.modb, b0], writes=[self.modb])
                kb.op("dve", lambda e, s_=s_, l=l: e.scalar_tensor_tensor(
                    out=self.gsM[:, l, :, s_], in0=self.modT[:, l, 32:40, s_], scalar=1.0, in1=gmlp[:, l, :],
                    op0=ALU.add, op1=ALU.mult), reads=[self.modb, b0], writes=[self.modb])
                if l % 3 == 0:
                    kb.op("dve", lambda e, s_=s_, l=l: e.tensor_tensor(
                        out=self.gb[:, l, :, s_], in0=self.modT[:, l, 16:24, s_], in1=fb[:, l // 3, :], op=ALU.mult),
                        reads=[self.modb, b0], writes=[self.modb])
        kb.barrier()
        self.release(mk)

    def shA(self, l, k, s):
        return self.modT[:, l, k, s:s + 1]

    def gateA(self, l, k, s):
        return self.modT[:, l, 16 + k, s:s + 1]

    def shM(self, l, k, s):
        return self.modT[:, l, 24 + k, s:s + 1]

    def gateM(self, l, k, s):
        return self.modT[:, l, 40 + k, s:s + 1]

    def alloc_common(self, n_xt=2):
        c = {}
        xts = [self.sb("xt%d" % i, [128, 8, T], F32) for i in range(n_xt)]
        c["xt"] = Rot([x[:] for x in xts], "xt")
        self._sq = self.sb("sq", [128, 8, T], BF16)
        c["sq"] = (self._sq[:], Buf("sq"))
        self._st = self.sb("stt", [128, 3, T], F32)
        c["lnv"] = (self._st[:, 0, :], Buf("lnv"))
        c["rstd"] = (self._st[:, 1, :], Buf("rstd"))
        tm = self.sb("tmpm", [128, 2, T], F32)
        c["tmp"] = Rot([tm[:, i, :] for i in range(2)], "tmp")
        self._hb = self.sb("hb", [128, 8, T], BF16)
        c["hb"] = (self._hb[:], Buf("hb"))
        return c

    def load_xt(self, c, l, t, from_input):
        kb = self.kb
        xt, xb = c["xt"].next()
        ti = tinfo(t)
        if not from_input:
            kb.dma("sp", xt, self.XT[:, :, ti["tok0"]:ti["tok0"] + T].rearrange("m p t -> p m t"),
                   reads=[self.XTb[t]], writes=[xb])
            return xt, xb
        xin, xinb = c["xin"].next()
        if ti["ctx"]:
            src = self.ctx_in[ti["lb"]].rearrange("(u p) d -> p u d", p=128)
        else:
            src = self.x_in[ti["lb"], ti["pos"]:ti["pos"] + T, :].rearrange("(u p) d -> p u d", p=128)
        kb.dma("sp", xin, src, writes=[xinb])
        for half in range(2):
            pap, pb = c["ptr"].next()
            pv = pap.rearrange("p (m t) -> p m t", t=T)
            for mm_ in range(4):
                m = half * 4 + mm_
                for u in range(2):
                    lastone = (mm_ == 3 and u == 1)
                    kb.op("pe", lambda e, m=m, u=u, mm_=mm_: e.transpose(
                        out=pv[:, mm_, u * 128:(u + 1) * 128], in_=xin[:, u, m * 128:(m + 1) * 128],
                        identity=self.ident[:]),
                        reads=[xinb, self.cb] if (mm_ == 0 and u == 0) else (),
                        writes=[pb] if (mm_ == 0 and u == 0) else (), inc=lastone)
            eng = "dve" if half == 0 else "act"
            if eng == "dve":
                kb.op("dve", lambda e, half=half: e.tensor_copy(out=xt[:, half * 4:half * 4 + 4, :], in_=pv),
                      reads=[pb], writes=[xb])
            else:
                kb.op("act", lambda e, half=half: e.copy(out=xt[:, half * 4:half * 4 + 4, :], in_=pv),
                      reads=[pb], writes=[xb])
        return xt, xb

    def stats(self, c, src_ap, src_buf, nchunk, pstat, inv_n):
        kb = self.kb
        sq, sqb = c["sq"]
        kb.op("act", lambda e: e.activation(out=sq[:, 0:nchunk, :], in_=src_ap, func=AF.Square),
              reads=[src_buf], writes=[sqb])
        pap, pb = pstat
        kb.mm(pap, pb, [(self.ones[:], sq[:, k, :]) for k in range(nchunk)], reads=[sqb, self.cb])
        lnv, lb_ = c["lnv"]
        rstd, rb = c["rstd"]
        kb.op("act", lambda e: e.activation(out=lnv, in_=pap, func=AF.Ln, bias=EPS, scale=inv_n),
              reads=[pb], writes=[lb_])
        kb.op("act", lambda e: e.activation(out=rstd, in_=lnv, func=AF.Exp, scale=-0.5),
              reads=[lb_], writes=[rb])
        return rstd, rb

    def modulate(self, c, xt, xb, l, cond, which, pstat):
        kb = self.kb
        rstd, rb = self.stats(c, xt, xb, 8, pstat, 1.0 / D)
        hb, hbb = c["hb"]
        gs = self.gsA if which == "a" else self.gsM
        for k in range(8):
            tmp, tb = c["tmp"].next()
            kb.op("dve", lambda e, k=k, tmp=tmp: e.tensor_tensor(out=tmp, in0=xt[:, k, :], in1=rstd, op=ALU.mult),
                  reads=[xb, rb], writes=[tb])
            sh = self.shA(l, k, cond) if which == "a" else self.shM(l, k, cond)
            kb.op("act", lambda e, k=k, tmp=tmp, sh=sh: e.activation(
                out=hb[:, k, :], in_=tmp, func=AF.Identity, bias=sh, scale=gs[:, l, k, cond:cond + 1]),
                reads=[tb, self.modb], writes=[hbb])
        return hb, hbb

    def load_w_bf16(self, dst_ap, src_ap, buf, q="pool"):
        self.kb.dma("pool", dst_ap, src_ap, writes=[buf])

    def preg(self, lo, hi):
        return (self.ps[:, lo:hi], [self.psh[i] for i in range(lo // 512, (hi - 1) // 512 + 1)])

    def prot(self, regs, name):
        aps, bufs = [], []
        for lo, hi in regs:
            a, b = self.preg(lo, hi)
            aps.append(a)
            bufs.append(b)
        return Rot(aps, name, bufs)

    def psum_slots(self, bank0, nslots, width, name):
        regs = [((bank0 + i) * 512, (bank0 + i) * 512 + width) for i in range(nslots)]
        assert bank0 + nslots <= 8
        return self.prot(regs, name)

    def fourier_front(self, l, w):
        kb = self.kb
        mk = self.mark()
        first = (l == 0)
        c = self.alloc_common()
        if first:
            xi = [self.sb("xin%d" % i, [128, 2, D], F32) for i in range(2)]
            c["xin"] = Rot([x[:] for x in xi], "xin")
            c["ptr"] = self.prot([(2048, 3072), (3072, 4096)], "ptr")
        pstat = self.preg(0, T)
        pz = self.prot([(512, 1024), (1024, 1536), (1536, 2048)], "pz")
        wcs = self.sb("wcs", [128, 8, 2048], BF16)
        wcsb = Buf("wcs")
        mk2 = self.mark()
        wf = self.sb("wf", [128, 8, D], F32)
        cr = self.sb("c256r", [128, 2, 2, 256], F32)
        wfb = Buf("wf")
        kb.dma("sp", wf[:], w.rearrange("(kc p) n -> p kc n", p=128), writes=[wfb])
        kb.dma("sp", cr[:, 0], self.cs["C256r"], writes=[wfb])
        kb.dma("sp", cr[:, 1], self.cs["S256r"], writes=[wfb])
        i = 0
        for tb_ in range(2):
            for g in range(4):
                for cc in range(2):
                    for nh in range(2):
                        pap, pb = pz.next()
                        pairs = [(cr[:, tb_, kc, cc * 128:(cc + 1) * 128], wf[:, g * 2 + kc, nh * 512:(nh + 1) * 512])
                                 for kc in range(2)]
                        kb.mm(pap, pb, pairs, reads=[wfb])
                        dst = wcs[:, g * 2 + cc, tb_ * 1024 + nh * 512: tb_ * 1024 + (nh + 1) * 512]
                        if i % 2 == 0:
                            kb.op("dve", lambda e, dst=dst, pap=pap: e.tensor_copy(out=dst, in_=pap),
                                  reads=[pb], writes=[wcsb])
                        else:
                            kb.op("act", lambda e, dst=dst, pap=pap: e.copy(out=dst, in_=pap),
                                  reads=[pb], writes=[wcsb])
                        i += 1
        kb.barrier()
        self.release(mk2)
        zs = [self.sb("zsb%d" % i, [128, 2, 2048], BF16) for i in range(2)]
        zrot = Rot([z[:] for z in zs], "zsb")
        tiles = list(range(NT)) if l == 0 else list(range(2, NT))
        for t in tiles:
            ti = tinfo(t)
            xt, xb = self.load_xt(c, l, t, first)
            if first:
                kb.dma("pool", self.XT[:, :, ti["tok0"]:ti["tok0"] + T].rearrange("m p t -> p m t"), xt,
                       reads=[xb], writes=[self.XTb[t]])
            hb, hbb = self.modulate(c, xt, xb, l, ti["cond"], "a", pstat)
            zsb, zb = zrot.next()
            i = 0
            for u in range(2):
                for nh in range(4):
                    pap, pb = pz.next()
                    pairs = [(hb[:, k, u * 128:(u + 1) * 128], wcs[:, k, nh * 512:(nh + 1) * 512]) for k in range(8)]
                    kb.mm(pap, pb, pairs, reads=[hbb, wcsb])
                    dst = zsb[:, u, nh * 512:(nh + 1) * 512]
                    if i % 2 == 0:
                        kb.op("dve", lambda e, dst=dst, pap=pap: e.tensor_copy(out=dst, in_=pap),
                              reads=[pb], writes=[zb])
                    else:
                        kb.op("pool", lambda e, dst=dst, pap=pap: None, reads=(), writes=()) if False else \
                            kb.op("act", lambda e, dst=dst, pap=pap: e.copy(out=dst, in_=pap), reads=[pb], writes=[zb])
                    i += 1
            kb.dma("pool", self.ZS[ti["tok0"]:ti["tok0"] + T, :].rearrange("(u p) n -> p u n", p=128), zsb,
                   reads=[zb], writes=[self.ZSb[t]])
        kb.barrier()
        self.release(mk)

    def fourier_mid(self, l):
        kb = self.kb
        mk = self.mark()
        ta = self.sb("ta", [128, 32, 3, 128], BF16)
        tbm = self.sb("tbm", [128, 2, 128], BF16)
        c256 = self.sb("c256s", [128, 2, 2, 256], BF16)
        tabb = Buf("tab")
        kb.dma("sp", ta[:], self.cs["TA"], writes=[tabb])
        kb.dma("sp", tbm[:], self.cs["TB"], writes=[tabb])
        kb.dma("sp", c256[:, 0], self.cs["C256s"], writes=[tabb])
        kb.dma("sp", c256[:, 1], self.cs["nS256s"], writes=[tabb])
        zq_t = [self.sb("zq%d" % i, [128, 2048], BF16) for i in range(2)]
        zq = Rot([z[:] for z in zq_t], "zq")
        vs_t = [self.sb("vsb%d" % i, [128, 2048], BF16) for i in range(2)]
        vsr = Rot([z[:] for z in vs_t], "vsb")
        vb_t = [self.sb("vb%d" % i, [128, 2048], BF16) for i in range(2)]
        vbr = Rot([z[:] for z in vb_t], "vb")
        ysb = self.sb("ysb", [128, 8, S], BF16)
        ysbb = Buf("ysb")
        ysb2 = self.sb("ysb2", [128, 8, S], BF16)
        ysb2b = Buf("ysb2")
        pa = self.prot([(0, 2048), (2048, 4096)], "pa")
        import os
        FM = int(os.environ.get("FM_STOP", "9"))
        for lb in range(NB):
            if FM <= 1:
                break
            lt0 = 2 + 16 * lb
            zsbufs = [self.ZSb[t] for t in range(lt0, lt0 + 16)]
            base = lat_tok(lb)
            for q in range(32):
                zt, zb = zq.next()
                for al in range(4):
                    r0 = base + 4 * q + al
                    kb.dma("sp", zt[32 * al:32 * al + 32, :], self.ZS[r0:r0 + 31 * 128 + 1:128, :], reads=zsbufs, writes=[zb])
                pap, pb = pa.next()
                for o in range(4):
                    nh = o % 2
                    A = zt[:, nh * 512:(nh + 1) * 512]
                    B = zt[:, 1024 + nh * 512:1024 + (nh + 1) * 512]
                    if o < 2:
                        pairs = [(ta[:, q, 0, :], A), (ta[:, q, 2, :], B)]
                    else:
                        pairs = [(ta[:, q, 0, :], B), (ta[:, q, 1, :], A)]
                    for i, (lh, rh) in enumerate(pairs):
                        kb.op("pe", lambda e, lh=lh, rh=rh, i=i, o=o: e.matmul(
                            pap[:, o * 512:(o + 1) * 512], lhsT=lh, rhs=rh, start=(i == 0), stop=(i == 1)),
                            reads=[zb, tabb] if (o == 0 and i == 0) else (),
                            writes=[pb] if (o == 0 and i == 0) else (), inc=(o == 3 and i == 1))
                vt, vb_ = vsr.next()
                kb.op("dve", lambda e, vt=vt, pap=pap: e.tensor_copy(out=vt[:, 0:1024], in_=pap[:, 0:1024]),
                      reads=[pb], writes=[vb_])
                kb.op("act", lambda e, vt=vt, pap=pap: e.copy(out=vt[:, 1024:2048], in_=pap[:, 1024:2048]),
                      reads=[pb], writes=[vb_])
                for al in range(4):
                    a = 4 * q + al
                    kb.dma("pool", self.VSF[lb, a:a + 31 * 128 + 1:128, :], vt[32 * al:32 * al + 32, :],
                           reads=[vb_], writes=[self.VSFb[lb]])
            if FM <= 2:
                continue
            ysv = ysb[:].rearrange("p m (a r) -> p m a r", r=32)
            pbr = self.prot([(0, 1024), (1024, 2048), (2048, 3072), (3072, 4096)], "pbr")
            for bp in range(32):
                vt, vb_ = vbr.next()
                kb.dma("sp", vt, self.VSF[lb, bp * 128:(bp + 1) * 128, :], reads=[self.VSFb[lb]], writes=[vb_])
                pap, pb = pbr.next()
                pv = pap.rearrange("p (m a) -> p m a", a=128)
                for m in range(8):
                    for i in range(2):
                        kb.op("pe", lambda e, m=m, i=i, vt=vt, pv=pv: e.matmul(
                            pv[:, m, :], lhsT=vt[:, i * 1024 + m * 128: i * 1024 + (m + 1) * 128], rhs=tbm[:, i, :],
                            start=(i == 0), stop=(i == 1)),
                            reads=[vb_, tabb] if (m == 0 and i == 0) else (),
                            writes=[pb] if (m == 0 and i == 0) else (), inc=(m == 7 and i == 1))
                if bp % 2 == 0:
                    kb.op("dve", lambda e, bp=bp, pv=pv: e.tensor_copy(out=ysb[:, :, bp * 128:(bp + 1) * 128], in_=pv),
                          reads=[pb], writes=[ysbb])
                else:
                    kb.op("act", lambda e, bp=bp, pv=pv: e.copy(out=ysb[:, :, bp * 128:(bp + 1) * 128], in_=pv),
                          reads=[pb], writes=[ysbb])
            for m in range(8):
                src = ysb[:, m, :].rearrange("p (b a) -> p a b", a=128)
                dst = ysb2[:, m, :].rearrange("p (a b) -> p a b", b=32)
                eng = ("dve", "pool", "act")[m % 3]
                if eng == "act":
                    kb.op("act", lambda e, src=src, dst=dst: e.copy(out=dst, in_=src), reads=[ysbb], writes=[ysb2b])
                else:
                    kb.op(eng, lambda e, src=src, dst=dst: e.tensor_copy(out=dst, in_=src), reads=[ysbb], writes=[ysb2b])
            for m in range(8):
                if FM <= 3:
                    break
                kb.dma("pool", self.MT[m, :, base:base + S], ysb2[:, m, :], reads=[ysb2b],
                       writes=[self.MTb[t] for t in range(lt0, lt0 + 16)])
        if l == 0 and FM >= 5:
            kb.barrier()
            zc_t = self.sb("zc", [128, 2, 2048], BF16)
            yc = self.sb("yc", [128, 8, C], BF16)
            pcr = self.psum_slots(0, 4, C, "pc")
            for lb in range(NB):
                zcb = Buf("zc")
                ycb = Buf("yc")
                t = lb
                kb.dma("sp", zc_t[:], self.ZS[ctx_tok(lb):ctx_tok(lb) + C, :].rearrange("(u p) n -> p u n", p=128),
                       reads=[self.ZSb[t]], writes=[zcb])
                for m in range(8):
                    pap, pb = pcr.next()
                    pairs = []
                    for u in range(2):
                        pairs.append((zc_t[:, u, m * 128:(m + 1) * 128], c256[:, 0, u, :]))
                        pairs.append((zc_t[:, u, 1024 + m * 128:1024 + (m + 1) * 128], c256[:, 1, u, :]))
                    kb.mm(pap, pb, pairs, reads=[zcb, tabb])
                    kb.op("dve", lambda e, m=m, pap=pap: e.tensor_copy(out=yc[:, m, :], in_=pap), reads=[pb], writes=[ycb])
                if int(os.environ.get("FM_SUB", "0")) != 7:
                    kb.dma("pool", self.MT[:, :, ctx_tok(lb):ctx_tok(lb) + C].rearrange("m p t -> p m t"), yc[:],
                           reads=[ycb], writes=[self.MTb[t]])
                kb.barrier()
        kb.barrier()
        self.release(mk)

    def tail(self, l, wo_dram, last):
        kb = self.kb
        mk = self.mark()
        c = self.alloc_common()
        fourier = wo_dram is None
        w1 = self.sb("w1", [128, 8, 4 * D], BF16)
        w2 = self.sb("w2", [128, 32, D], BF16)
        w1b, w2b = Buf("w1"), Buf("w2")
        wob = Buf("wo")
        if not fourier:
            wo = self.sb("wo", [128, 8, D], BF16)
            self.load_w_bf16(wo[:], wo_dram.rearrange("(kc p) n -> p kc n", p=128), wob)
        s1 = self.cs
        src1 = None
        for k in range(8):
            self.load_w_bf16(w1[:, k, :], self.din["mlp_w1"][l, k * 128:(k + 1) * 128, :], w1b)
        for f4 in range(8):
            self.load_w_bf16(w2[:, f4 * 4:(f4 + 1) * 4, :],
                             self.din["mlp_w2"][l, f4 * 512:(f4 + 1) * 512, :].rearrange("(f p) n -> p f n", p=128), w2b)
        mt_t = self.sb("mt", [128, 8, T], BF16)
        mtb = Buf("mt")
        a_t = self.sb("a_t", [128, 32, T], BF16)
        ab = Buf("a")
        r_t = self.sb("r_t", [128, 3, T], F32)
        rrot = Rot([r_t[:, i, :] for i in range(3)], "r")
        pstat = self.preg(0, T)
        pstat2 = self.preg(T, 2 * T)
        pop = self.psum_slots(1, 2, T, "pop")
        ph = self.psum_slots(3, 3, T, "ph")
        py = self.psum_slots(6, 2, T, "py")
        if last:
            of_t = self.sb("of", [128, 8, T], F32)
            ofb = Buf("of")
            os_t = [self.sb("osb%d" % i, [128, D], F32) for i in range(2)]
            osr = Rot([o[:] for o in os_t], "osb")
            assert fourier
            ptr = self.prot([(512, 1024), (1024, 1536)], "ptr2")
        tiles = list(range(NT))
        if l >= 2:
            tiles = list(range(2, NT))
        state = {}

        def part1(t):
            ti = tinfo(t)
            cond = ti["cond"]
            xt, xb = self.load_xt(c, l, t, False)
            kb.dma("sp", mt_t[:], self.MT[:, :, ti["tok0"]:ti["tok0"] + T].rearrange("m p t -> p m t"),
                   reads=[self.MTb[t]], writes=[mtb])
            for m in range(8):
                if fourier:
                    kb.op("dve", lambda e, m=m: e.scalar_tensor_tensor(
                        out=xt[:, m, :], in0=mt_t[:, m, :], scalar=self.gateA(l, m, cond), in1=xt[:, m, :],
                        op0=ALU.mult, op1=ALU.add), reads=[mtb, xb, self.modb], writes=[xb])
                    kb.op("pool", lambda e, m=m: e.tensor_scalar(
                        out=xt[:, m, :], in0=xt[:, m, :], scalar1=self.gb[:, l, m, cond:cond + 1], scalar2=None,
                        op0=ALU.add), reads=[xb, self.modb], writes=[xb])
                else:
                    pap, pb = pop.next()
                    kb.mm(pap, pb, [(wo[:, k, m * 128:(m + 1) * 128], mt_t[:, k, :]) for k in range(8)],
                          reads=[mtb, wob])
                    kb.op("dve", lambda e, m=m, pap=pap: e.scalar_tensor_tensor(
                        out=xt[:, m, :], in0=pap, scalar=self.gateA(l, m, cond), in1=xt[:, m, :],
                        op0=ALU.mult, op1=ALU.add), reads=[pb, xb, self.modb], writes=[xb])
            hb, hbb = self.modulate(c, xt, xb, l, cond, "m", pstat)
            state[t] = (xt, xb, hb, hbb)

        def part2(t):
            xt, xb, hb, hbb = state[t]
            for f in range(32):
                pap, pb = ph.next()
                kb.mm(pap, pb, [(w1[:, k, f * 128:(f + 1) * 128], hb[:, k, :]) for k in range(8)], reads=[hbb, w1b])
                r, rb = rrot.next()
                kb.op("act", lambda e, r=r, pap=pap: e.activation(out=r, in_=pap, func=AF.Relu), reads=[pb], writes=[rb])
                eng = "dve" if f % 2 == 0 else "pool"
                kb.op(eng, lambda e, r=r, f=f: e.tensor_tensor(out=a_t[:, f, :], in0=r, in1=r, op=ALU.mult),
                      reads=[rb], writes=[ab])

        def part3(t):
            ti = tinfo(t)
            cond = ti["cond"]
            xt, xb, hb, hbb = state.pop(t)
            for m in range(8):
                pap, pb = py.next()
                kb.mm(pap, pb, [(w2[:, f, m * 128:(m + 1) * 128], a_t[:, f, :]) for f in range(32)], reads=[ab, w2b])
                kb.op("dve", lambda e, m=m, pap=pap: e.scalar_tensor_tensor(
                    out=xt[:, m, :], in0=pap, scalar=self.gateM(l, m, cond), in1=xt[:, m, :],
                    op0=ALU.mult, op1=ALU.add), reads=[pb, xb, self.modb], writes=[xb])
            SAMP = {0: 0, 2: 1, 17: 2, 33: 3}
            if self.debug and t in SAMP:
                kb.dma("pool", self.XS[l][:, :, SAMP[t] * T:(SAMP[t] + 1) * T].rearrange("m p t -> p m t"), xt,
                       reads=[xb], writes=[self.XSb])
            if not last:
                kb.dma("pool", self.XT[:, :, ti["tok0"]:ti["tok0"] + T].rearrange("m p t -> p m t"), xt,
                       reads=[xb], writes=[self.XTb[t]])
                return
            if self.debug:
                kb.dma("pool", self.XT[:, :, ti["tok0"]:ti["tok0"] + T].rearrange("m p t -> p m t"), xt,
                       reads=[xb], writes=[self.XTb[t]])
            rstd, rb = self.stats(c, xt, xb, 8, pstat2, 1.0 / D)
            for m in range(8):
                kb.op("dve", lambda e, m=m: e.scalar_tensor_tensor(
                    out=of_t[:, m, :], in0=xt[:, m, :], scalar=self.gfin[:, m:m + 1], in1=rstd,
                    op0=ALU.mult, op1=ALU.mult), reads=[xb, rb, self.cb], writes=[ofb])
            for u in range(2):
                osb, osbb = osr.next()
                for half in range(2):
                    pap, pb = ptr.next()
                    for mm_ in range(4):
                        m = half * 4 + mm_
                        kb.op("pe", lambda e, m=m, mm_=mm_, u=u, pap=pap: e.transpose(
                            out=pap[:, mm_ * 128:(mm_ + 1) * 128], in_=of_t[:, m, u * 128:(u + 1) * 128],
                            identity=self.ident[:]),
                            reads=[ofb, self.cb] if mm_ == 0 else (), writes=[pb] if mm_ == 0 else (), inc=(mm_ == 3))
                    if half == 0:
                        kb.op("dve", lambda e, pap=pap, osb=osb: e.tensor_copy(out=osb[:, 0:512], in_=pap),
                              reads=[pb], writes=[osbb])
                    else:
                        kb.op("act", lambda e, pap=pap, osb=osb: e.copy(out=osb[:, 512:1024], in_=pap),
                              reads=[pb], writes=[osbb])
                p0 = ti["pos"] + u * 128
                kb.dma("pool", self.out[ti["lb"], p0:p0 + 128, :], osb, reads=[osbb], writes=[self.outb])

        part1(tiles[0])
        for i, t in enumerate(tiles):
            part2(t)
            if i + 1 < len(tiles):
                part1(tiles[i + 1])
            part3(t)
        kb.barrier()
        self.release(mk)

    def diff_front(self, l, wqkv_dram):
        kb = self.kb
        mk = self.mark()
        c = self.alloc_common()
        wq = self.sb("wqkv", [128, 8, 3 * D], BF16)
        wqb = Buf("wqkv")
        for k in range(8):
            self.load_w_bf16(wq[:, k, :], wqkv_dram[k * 128:(k + 1) * 128, :], wqb)
        perm = self.sb("permD", [128, 128], BF16)
        kb.dma("sp", perm[:], self.cs["permD"], writes=[wqb])
        cs_t = [self.sb("csD%d" % i, [128, 2, T], F32) for i in range(2)]
        csr = Rot([x[:] for x in cs_t], "csD")
        qk_t = [self.sb("qk%d" % i, [128, 16, T], BF16) for i in range(2)]
        qkr = Rot([x[:] for x in qk_t], "qk")
        vs_t = [self.sb("vsd%d" % i, [128, 2, D], BF16) for i in range(2)]
        vsr = Rot([x[:] for x in vs_t], "vsd")
        qraw_t = self.sb("qraw", [128, 3, T], BF16)
        qrr = Rot([qraw_t[:, i, :] for i in range(3)], "qraw")
        t1_t = self.sb("t1", [128, 3, T], F32)
        t1r = Rot([t1_t[:, i, :] for i in range(3)], "t1")
        t2_t = self.sb("t2", [128, 3, T], F32)
        t2r = Rot([t2_t[:, i, :] for i in range(3)], "t2")
        pstat = self.preg(0, T)
        pq = self.psum_slots(1, 3, T, "pq")
        pr = self.psum_slots(4, 2, T, "pr")
        pv = self.prot([(3072, 3584), (3584, 4096)], "pv")
        for t in range(NT):
            ti = tinfo(t)
            xt, xb = self.load_xt(c, l, t, False)
            hb, hbb = self.modulate(c, xt, xb, l, ti["cond"], "a", pstat)
            qk, qkb = qkr.next()
            if not ti["ctx"]:
                cst, csb = csr.next()
                kb.dma("sp", cst[:, 0, :], self.cs["cosD"][:, ti["pos"]:ti["pos"] + T], writes=[csb])
                kb.dma("sp", cst[:, 1, :], self.cs["sinD"][:, ti["pos"]:ti["pos"] + T], writes=[csb])
            for j in range(16):
                pap, pb = pq.next()
                kb.mm(pap, pb, [(wq[:, k, j * 128:(j + 1) * 128], hb[:, k, :]) for k in range(8)], reads=[hbb, wqb])
                if ti["ctx"]:
                    kb.op("act", lambda e, j=j, pap=pap: e.copy(out=qk[:, j, :], in_=pap), reads=[pb], writes=[qkb])
                    continue
                qr, qrb = qrr.next()
                kb.op("act", lambda e, qr=qr, pap=pap: e.copy(out=qr, in_=pap), reads=[pb], writes=[qrb])
                pap2, pb2 = pr.next()
                kb.mm(pap2, pb2, [(perm[:], qr)], reads=[qrb, wqb])
                t1, t1b = t1r.next()
                t2, t2b = t2r.next()
                kb.op("pool", lambda e, t1=t1, qr=qr: e.tensor_tensor(out=t1, in0=qr, in1=cst[:, 0, :], op=ALU.mult),
                      reads=[qrb, csb], writes=[t1b])
                kb.op("dve", lambda e, t2=t2, pap2=pap2: e.tensor_tensor(out=t2, in0=pap2, in1=cst[:, 1, :], op=ALU.mult),
                      reads=[pb2, csb], writes=[t2b])
                kb.op("pool", lambda e, t1=t1, t2=t2, j=j: e.tensor_tensor(out=qk[:, j, :], in0=t1, in1=t2, op=ALU.add),
                      reads=[t1b, t2b], writes=[qkb])
            vs, vsb_ = vsr.next()
            i = 0
            for u in range(2):
                for nh in range(2):
                    pap, pb = pv.next()
                    kb.mm(pap, pb, [(hb[:, k, u * 128:(u + 1) * 128], wq[:, k, 2048 + nh * 512:2048 + (nh + 1) * 512])
                                    for k in range(8)], reads=[hbb, wqb])
                    dst = vs[:, u, nh * 512:(nh + 1) * 512]
                    if i % 2 == 0:
                        kb.op("dve", lambda e, dst=dst, pap=pap: e.tensor_copy(out=dst, in_=pap), reads=[pb], writes=[vsb_])
                    else:
                        kb.op("act", lambda e, dst=dst, pap=pap: e.copy(out=dst, in_=pap), reads=[pb], writes=[vsb_])
                    i += 1
            for j4 in range(4):
                kb.dma("pool", self.QK[4 * j4:4 * j4 + 4, :, ti["tok0"]:ti["tok0"] + T].rearrange("j p t -> p j t"),
                       qk[:, 4 * j4:4 * j4 + 4, :], reads=[qkb], writes=[self.QKb[t]])
            kb.dma("pool", self.VS[ti["tok0"]:ti["tok0"] + T, :].rearrange("(u p) n -> p u n", p=128), vs,
                   reads=[vsb_], writes=[self.VSb[t]])
        kb.barrier()
        self.release(mk)

    def diff_mid(self, l, dlam, dsub):
        kb = self.kb
        mk = self.mark()
        lam_init = 0.8 - 0.6 * math.exp(-0.3 * l)
        lm = self.sb("lm", [128, 4, 64], F32)
        lmb = Buf("lm")
        kb.dma("sp", lm[:], dlam, writes=[lmb])
        sub = self.sb("subg", [128, 2], F32)
        kb.dma("sp", sub[:, 0:1], dsub, writes=[lmb])
        pr_ = self.sb("lpr", [128, 2, 64], F32)
        sc_ = self.sb("lsc", [128, 4], F32)
        scb = Buf("lsc")
        for i in range(2):
            kb.op("dve", lambda e, i=i: e.tensor_tensor(out=pr_[:, i, :], in0=lm[:, 2 * i, :], in1=lm[:, 2 * i + 1, :],
                                                        op=ALU.mult), reads=[lmb], writes=[scb])
            kb.op("dve", lambda e, i=i: e.reduce_sum(out=sc_[:, i:i + 1], in_=pr_[:, i, :], axis=AX.X),
                  reads=[scb], writes=[scb])
        kb.op("act", lambda e: e.activation(out=sc_[:, 0:2], in_=sc_[:, 0:2], func=AF.Exp), reads=[scb], writes=[scb])
        kb.op("dve", lambda e: e.tensor_tensor(out=sc_[:, 2:3], in0=sc_[:, 0:1], in1=sc_[:, 1:2], op=ALU.subtract),
              reads=[scb], writes=[scb])
        kb.op("dve", lambda e: e.tensor_scalar(out=sc_[:, 3:4], in0=sc_[:, 2:3], scalar1=lam_init, scalar2=-1.0,
                                               op0=ALU.add, op1=ALU.mult), reads=[scb], writes=[scb])
        kb.op("dve", lambda e: e.tensor_scalar(out=sub[:, 1:2], in0=sub[:, 0:1], scalar1=(1.0 - lam_init), scalar2=None,
                                               op0=ALU.mult), reads=[lmb], writes=[scb])
        nlam = sc_[:, 3:4]
        gsub = sub[:, 1:2]
        NK = 34
        kt_t = [self.sb("kT%d" % i, [128, NK * 128], BF16) for i in range(2)]
        ktr = Rot([x[:] for x in kt_t], "kT")
        qt_t = [self.sb("qT%d" % i, [128, NK * 128], BF16) for i in range(2)]
        qtr = Rot([x[:] for x in qt_t], "qT")
        vh_t = [self.sb("vh%d" % i, [128, NK, 128], BF16) for i in range(2)]
        vhr = Rot([x[:] for x in vh_t], "vh")
        p_t = self.sb("pT", [128, 4, 512], BF16)
        prot = Rot([p_t[:, i, :] for i in range(4)], "pT")
        fin = self.sb("fin", [128, 5, 512], F32)
        finb = [Buf("fin%d" % i) for i in range(5)]
        sqf = self.sb("sqf", [128, 512], BF16)
        sqfb = Buf("sqf")
        ob_t = self.sb("obt", [128, 2, 512], BF16)
        obr = Rot([ob_t[:, i, :] for i in range(2)], "ob")
        psS = self.prot([(0, 512), (512, 1024), (1024, 1536)], "psS")
        pO = [self.preg(1536 + 512 * i, 2048 + 512 * i) for i in range(4)]
        pF = self.preg(3584, 4096)
        scale = 64 ** -0.5
        for lb in range(NB):
            tl = [lb] + list(range(2 + 16 * lb, 18 + 16 * lb))
            for hh in range(8):
                kT, kTb = ktr.next()
                qT, qTb = qtr.next()
                vh, vhb = vhr.next()
                for (dst, dbuf, row) in ((kT, kTb, 8 + hh), (qT, qTb, hh)):
                    kb.dma("sp", dst[:, 0:C], self.QK[row, :, ctx_tok(lb):ctx_tok(lb) + C],
                           reads=[self.QKb[t] for t in tl], writes=[dbuf])
                    kb.dma("sp", dst[:, C:C + S], self.QK[row, :, lat_tok(lb):lat_tok(lb) + S],
                           reads=[self.QKb[t] for t in tl], writes=[dbuf])
                kb.dma("sp", vh[:, 0:2, :],
                       self.VS[ctx_tok(lb):ctx_tok(lb) + C, hh * 128:(hh + 1) * 128].rearrange("(k p) e -> p k e", p=128),
                       reads=[self.VSb[t] for t in tl], writes=[vhb])
                for k8 in range(4):
                    r0 = lat_tok(lb) + k8 * 1024
                    kb.dma("sp", vh[:, 2 + 8 * k8:10 + 8 * k8, :],
                           self.VS[r0:r0 + 1024, hh * 128:(hh + 1) * 128].rearrange("(k p) e -> p k e", p=128),
                           reads=[self.VSb[t] for t in tl], writes=[vhb])
                for qt in range(9):
                    if qt == 0:
                        q0, nq, nkt = 0, C, 2
                    else:
                        q0, nq, nkt = C + (qt - 1) * 512, 512, NK
                    ntile = 2 * nkt
                    sl = {}

                    def emit_S(i):
                        kt, mp = i // 2, i % 2
                        pap, pb = psS.next()
                        lo = mp * 64
                        kb.mm(pap[:, 0:nq], pb, [(kT[lo:lo + 64, kt * 128:(kt + 1) * 128], qT[lo:lo + 64, q0:q0 + nq])],
                              reads=[kTb, qTb])
                        sl[i] = (pap, pb)

                    emit_S(0)
                    emit_S(1)
                    for i in range(ntile):
                        kt, mp = i // 2, i % 2
                        pap, pb = sl.pop(i)
                        pt, ptb = prot.next()
                        kb.op("act", lambda e, pt=pt, pap=pap: e.activation(out=pt[:, 0:nq], in_=pap[:, 0:nq], func=AF.Exp,
                                                                           scale=scale), reads=[pb], writes=[ptb])
                        if i + 2 < ntile:
                            emit_S(i + 2)
                        oap, obuf = pO[2 * mp]
                        lap, lbuf = pO[2 * mp + 1]
                        first, lastk = (kt == 0), (kt == nkt - 1)
                        kb.op("pe", lambda e, oap=oap, pt=pt, kt=kt, first=first, lastk=lastk: e.matmul(
                            oap[:, 0:nq], lhsT=vh[:, kt, :], rhs=pt[:, 0:nq], start=first, stop=lastk),
                            reads=[ptb, vhb], writes=[obuf], inc=False)
                        kb.op("pe", lambda e, lap=lap, pt=pt, first=first, lastk=lastk: e.matmul(
                            lap[:, 0:nq], lhsT=self.ones[:], rhs=pt[:, 0:nq], start=first, stop=lastk),
                            reads=[ptb, self.cb], writes=[lbuf], inc=True)
                    r1, r2, o1, o2, rs = [fin[:, i, 0:nq] for i in range(5)]
                    kb.op("dve", lambda e: e.reciprocal(out=r1, in_=pO[1][0][:, 0:nq]), reads=[pO[1][1]], writes=[finb[0]])
                    kb.op("dve", lambda e: e.reciprocal(out=r2, in_=pO[3][0][:, 0:nq]), reads=[pO[3][1]], writes=[finb[1]])
                    kb.op("pool", lambda e: e.tensor_scalar(out=r2, in0=r2, scalar1=nlam, scalar2=None, op0=ALU.mult),
                          reads=[finb[1], scb], writes=[finb[1]])
                    kb.op("dve", lambda e: e.tensor_tensor(out=o1, in0=pO[0][0][:, 0:nq], in1=r1, op=ALU.mult),
                          reads=[pO[0][1], finb[0]], writes=[finb[2]])
                    kb.op("dve", lambda e: e.tensor_tensor(out=o2, in0=pO[2][0][:, 0:nq], in1=r2, op=ALU.mult),
                          reads=[pO[2][1], finb[1]], writes=[finb[3]])
                    kb.op("pool", lambda e: e.tensor_tensor(out=o1, in0=o1, in1=o2, op=ALU.add),
                          reads=[finb[2], finb[3]], writes=[finb[2]])
                    kb.op("act", lambda e: e.activation(out=sqf[:, 0:nq], in_=o1, func=AF.Square),
                          reads=[finb[2]], writes=[sqfb])
                    kb.mm(pF[0][:, 0:nq], pF[1], [(self.ones[:], sqf[:, 0:nq])], reads=[sqfb, self.cb])
                    kb.op("act", lambda e: e.activation(out=rs, in_=pF[0][:, 0:nq], func=AF.Ln, bias=EPS, scale=1.0 / 128),
                          reads=[pF[1]], writes=[finb[4]])
                    kb.op("act", lambda e: e.activation(out=rs, in_=rs, func=AF.Exp, scale=-0.5),
                          reads=[finb[4]], writes=[finb[4]])
                    ob, obb = obr.next()
                    kb.op("dve", lambda e, ob=ob: e.scalar_tensor_tensor(out=ob[:, 0:nq], in0=o1, scalar=gsub, in1=rs,
                                                                         op0=ALU.mult, op1=ALU.mult),
                          reads=[finb[2], finb[4], scb], writes=[obb])
                    tok = (ctx_tok(lb) if qt == 0 else lat_tok(lb) + (qt - 1) * 512)
                    kb.dma("pool", self.MT[hh, :, tok:tok + nq], ob[:, 0:nq], reads=[obb],
                           writes=[self.MTb[t] for t in tl])
        kb.barrier()
        self.release(mk)

    def mla_front(self, l, wd_dram, mgq, mgkv, wuq_dram, wukv_dram):
        kb = self.kb
        mk = self.mark()
        c = self.alloc_common()
        wb = Buf("mlaw")
        wd = self.sb("wd", [128, 8, 416], BF16)
        self.load_w_bf16(wd[:], wd_dram.rearrange("(kc p) n -> p kc n", p=128), wb)
        wuq = self.sb("wuq", [128, 2, 1536], BF16)
        self.load_w_bf16(wuq[:], wuq_dram.rearrange("(kc p) n -> p kc n", p=128), wb)
        wuk = self.sb("wuk", [128, 16, 64], BF16)
        wuv = self.sb("wuv", [128, 16, 64], BF16)
        src = wukv_dram.rearrange("k (h t e) -> k h t e", t=2, e=64)
        self.load_w_bf16(wuk[:], src[:, :, 0, :], wb)
        self.load_w_bf16(wuv[:], src[:, :, 1, :], wb)
        gq = self.sb("gq", [128, 2], F32)
        gkv = self.sb("gkv", [128, 1], F32)
        kb.dma("sp", gq[:], mgq, writes=[wb])
        kb.dma("sp", gkv[:], mgkv, writes=[wb])
        permM = self.sb("permM", [96, 96], BF16)
        permK = self.sb("permK", [32, 32], BF16)
        kb.dma("sp", permM[:], self.cs["permM"], writes=[wb])
        kb.dma("sp", permK[:], self.cs["permK"], writes=[wb])
        csm_t = [self.sb("csM%d" % i, [96, 2, T], F32) for i in range(2)]
        csmr = Rot([x[:] for x in csm_t], "csM")
        csk_t = [self.sb("csK%d" % i, [32, 2, T], F32) for i in range(2)]
        cskr = Rot([x[:] for x in csk_t], "csK")
        cqn = self.sb("cqn", [128, 3, T], BF16)
        cqnb = Buf("cqn")
        rsq = self.sb("rsq", [128, 2, T], F32)
        rsqb = [Buf("rsq0"), Buf("rsq1")]
        kr_t = self.sb("kr", [32, 2, T], BF16)
        krb = [Buf("kr0"), Buf("kr1")]
        qs_t = [self.sb("qsb%d" % i, [96, 16, T], BF16) for i in range(2)]
        qsr = Rot([x[:] for x in qs_t], "qsb")
        qraw_t = self.sb("qrawm", [96, 3, T], BF16)
        qrr = Rot([qraw_t[:, i, :] for i in range(3)], "qrawm")
        t1_t = self.sb("t1m", [96, 3, T], F32)
        t1r = Rot([t1_t[:, i, :] for i in range(3)], "t1m")
        t2_t = self.sb("t2m", [96, 3, T], F32)
        t2r = Rot([t2_t[:, i, :] for i in range(3)], "t2m")
        kn_t = [self.sb("kn%d" % i, [128, 8, T], BF16) for i in range(2)]
        knr = Rot([x[:] for x in kn_t], "kn")
        va_t = [self.sb("va%d" % i, [128, 2, 16, 128], BF16) for i in range(2)]
        vab = [Buf("va0"), Buf("va1")]
        for i in range(2):
            kb.op("pool", lambda e, i=i: e.memset(va_t[i][:], 1.0), writes=[vab[i]])
        pstat = self.preg(0, T)
        pstat2 = self.preg(512, 512 + T)
        pd = self.preg(1024, 2048)
        pq = self.psum_slots(4, 2, T, "pqm")
        pr = self.psum_slots(6, 1, T, "prm")
        pv = self.prot([(3584, 4096)], "pvm")
        tiles = list(range(NT))
        for it, t in enumerate(tiles):
            ti = tinfo(t)
            tok0 = ti["tok0"]
            xt, xb = self.load_xt(c, l, t, False)
            hb, hbb = self.modulate(c, xt, xb, l, ti["cond"], "a", pstat)
            pdv = pd[0].rearrange("p (c t) -> p c t", t=T)
            cols = [(0, 128), (128, 128), (256, 128), (384, 32)]
            for ci, (c0, cw) in enumerate(cols):
                for k in range(8):
                    kb.op("pe", lambda e, ci=ci, c0=c0, cw=cw, k=k: e.matmul(
                        pdv[0:cw, ci, :], lhsT=wd[:, k, c0:c0 + cw], rhs=hb[:, k, :], start=(k == 0), stop=(k == 7)),
                        reads=[hbb, wb] if (ci == 0 and k == 0) else (), writes=[pd[1]] if (ci == 0 and k == 0) else (),
                        inc=(ci == 3 and k == 7))
            sq, sqb = c["sq"]
            kb.op("act", lambda e: e.activation(out=sq[:, 0:3, :], in_=pdv[:, 0:3, :], func=AF.Square),
                  reads=[pd[1]], writes=[sqb])
            kb.mm(pstat[0], pstat[1], [(self.ones[:], sq[:, 0, :]), (self.ones[:], sq[:, 1, :])], reads=[sqb, self.cb])
            kb.mm(pstat2[0], pstat2[1], [(self.ones[:], sq[:, 2, :])], reads=[sqb, self.cb])
            for i, (pst, n) in enumerate(((pstat, 256), (pstat2, 128))):
                kb.op("act", lambda e, i=i, pst=pst, n=n: e.activation(out=rsq[:, i, :], in_=pst[0], func=AF.Ln, bias=EPS,
                                                                     scale=1.0 / n), reads=[pst[1]], writes=[rsqb[i]])
                kb.op("act", lambda e, i=i: e.activation(out=rsq[:, i, :], in_=rsq[:, i, :], func=AF.Exp, scale=-0.5),
                      reads=[rsqb[i]], writes=[rsqb[i]])
            for ci in range(3):
                g = gq[:, ci:ci + 1] if ci < 2 else gkv[:, 0:1]
                ri = 0 if ci < 2 else 1
                kb.op("dve", lambda e, ci=ci, g=g, ri=ri: e.scalar_tensor_tensor(
                    out=cqn[:, ci, :], in0=pdv[:, ci, :], scalar=g, in1=rsq[:, ri, :], op0=ALU.mult, op1=ALU.mult),
                    reads=[pd[1], rsqb[ri], wb], writes=[cqnb])
            kb.op("act", lambda e: e.copy(out=kr_t[:, 0, :], in_=pdv[0:32, 3, :]), reads=[pd[1]], writes=[krb[0]])
            if ti["ctx"]:
                kr_fin, kr_finb = kr_t[:, 0, :], krb[0]
            else:
                csk, cskb = cskr.next()
                kb.dma("sp", csk[:, 0, :], self.cs["cosK"][:, ti["pos"]:ti["pos"] + T], writes=[cskb])
                kb.dma("sp", csk[:, 1, :], self.cs["sinK"][:, ti["pos"]:ti["pos"] + T], writes=[cskb])
                pap2, pb2 = pr.next()
                kb.mm(pap2[0:32, :], pb2, [(permK[:], kr_t[:, 0, :])], reads=[krb[0], wb])
                t1, t1b = t1r.next()
                t2, t2b = t2r.next()
                kb.op("pool", lambda e, t1=t1: e.tensor_tensor(out=t1[0:32, :], in0=kr_t[:, 0, :], in1=csk[:, 0, :], op=ALU.mult),
                      reads=[krb[0], cskb], writes=[t1b])
                kb.op("dve", lambda e, t2=t2, pap2=pap2: e.tensor_tensor(out=t2[0:32, :], in0=pap2[0:32, :], in1=csk[:, 1, :],
                                                                         op=ALU.mult), reads=[pb2, cskb], writes=[t2b])
                kb.op("pool", lambda e, t1=t1, t2=t2: e.tensor_tensor(out=kr_t[:, 1, :], in0=t1[0:32, :], in1=t2[0:32, :],
                                                                     op=ALU.add), reads=[t1b, t2b], writes=[krb[1]])
                kr_fin, kr_finb = kr_t[:, 1, :], krb[1]
            kb.dma("pool", self.KR[:, tok0:tok0 + T], kr_fin, reads=[kr_finb], writes=[self.KRb[t]])
            if not ti["ctx"]:
                csm, csmb = csmr.next()
                kb.dma("sp", csm[:, 0, :], self.cs["cosM"][:, ti["pos"]:ti["pos"] + T], writes=[csmb])
                kb.dma("sp", csm[:, 1, :], self.cs["sinM"][:, ti["pos"]:ti["pos"] + T], writes=[csmb])
                qs, qsb_ = qsr.next()
                for h in range(16):
                    pap, pb = pq.next()
                    kb.mm(pap[0:96, :], pb, [(wuq[:, cc, h * 96:(h + 1) * 96], cqn[:, cc, :]) for cc in range(2)],
                          reads=[cqnb, wb])
                    qr, qrb = qrr.next()
                    kb.op("act", lambda e, qr=qr, pap=pap: e.copy(out=qr, in_=pap[0:96, :]), reads=[pb], writes=[qrb])
                    pap2, pb2 = pr.next()
                    kb.mm(pap2[0:96, :], pb2, [(permM[:], qr)], reads=[qrb, wb])
                    t1, t1b = t1r.next()
                    t2, t2b = t2r.next()
                    kb.op("pool", lambda e, t1=t1, qr=qr: e.tensor_tensor(out=t1, in0=qr, in1=csm[:, 0, :], op=ALU.mult),
                          reads=[qrb, csmb], writes=[t1b])
                    kb.op("dve", lambda e, t2=t2, pap2=pap2: e.tensor_tensor(out=t2, in0=pap2[0:96, :], in1=csm[:, 1, :],
                                                                             op=ALU.mult), reads=[pb2, csmb], writes=[t2b])
                    kb.op("pool", lambda e, t1=t1, t2=t2, h=h: e.tensor_tensor(out=qs[:, h, :], in0=t1, in1=t2, op=ALU.add),
                          reads=[t1b, t2b], writes=[qsb_])
                for h8 in range(2):
                    kb.dma("pool", self.QM[8 * h8:8 * h8 + 8, :, tok0:tok0 + T].rearrange("h p t -> p h t"),
                           qs[:, 8 * h8:8 * h8 + 8, :], reads=[qsb_], writes=[self.QMb[t]])
            kn, knb = knr.next()
            for hp in range(8):
                pap, pb = pq.next()
                kb.mm(pap, pb, [(wuk[:, 2 * hp:2 * hp + 2, :], cqn[:, 2, :])], reads=[cqnb, wb])
                if hp % 2 == 0:
                    kb.op("act", lambda e, hp=hp, pap=pap: e.copy(out=kn[:, hp, :], in_=pap), reads=[pb], writes=[knb])
                else:
                    kb.op("dve", lambda e, hp=hp, pap=pap: e.tensor_copy(out=kn[:, hp, :], in_=pap), reads=[pb], writes=[knb])
            ktv = self.KT.rearrange("(hp two) r t -> two r hp t", two=2)
            for two in range(2):
                kb.dma("pool", ktv[two][:, :, tok0:tok0 + T], kn[two * 64:(two + 1) * 64, :, :], reads=[knb],
                       writes=[self.KTb[t]])
            va, vab_ = va_t[it % 2], vab[it % 2]
            for u in range(2):
                for nh in range(2):
                    pap, pb = pv.next()
                    kb.mm(pap, pb, [(cqn[:, 2, u * 128:(u + 1) * 128], wuv[:, nh * 8:(nh + 1) * 8, :])], reads=[cqnb, wb])
                    pvv = pap.rearrange("p (h e) -> p h e", e=64)
                    kb.op("dve", lambda e, u=u, nh=nh, pvv=pvv, va=va: e.tensor_copy(
                        out=va[:, u, nh * 8:(nh + 1) * 8:2, 0:64], in_=pvv[:, 0:8:2, :]), reads=[pb], writes=[vab_])
                    kb.op("act", lambda e, u=u, nh=nh, pvv=pvv, va=va: e.copy(
                        out=va[:, u, nh * 8 + 1:(nh + 1) * 8:2, 64:128], in_=pvv[:, 1:8:2, :]), reads=[pb], writes=[vab_])
            kb.dma("pool", self.VA[tok0:tok0 + T, :].rearrange("(u p) n -> p u n", p=128),
                   va[:].rearrange("p u h e -> p u (h e)"), reads=[vab_], writes=[self.VAb[t]])
        kb.barrier()
        self.release(mk)

    def mla_mid(self, l):
        kb = self.kb
        mk = self.mark()
        NK = 34
        kt_t = [self.sb("kTm%d" % i, [96, NK * 128], BF16) for i in range(2)]
        ktr = Rot([x[:] for x in kt_t], "kTm")
        qt_t = [self.sb("qTm%d" % i, [96, S], BF16) for i in range(2)]
        qtr = Rot([x[:] for x in qt_t], "qTm")
        vh_t = [self.sb("vhm%d" % i, [128, NK, 128], BF16) for i in range(2)]
        vhr = Rot([x[:] for x in vh_t], "vhm")
        p_t = self.sb("pTm", [128, 4, 512], BF16)
        prot = Rot([p_t[:, i, :] for i in range(4)], "pTm")
        rs_t = self.sb("rsm", [128, 2, 512], F32)
        rsr = Rot([rs_t[:, i, :] for i in range(2)], "rsm")
        op_t = [self.sb("opair%d" % i, [128, S], BF16) for i in range(2)]
        opr = Rot([x[:] for x in op_t], "opair")
        psS = self.prot([(512 * i, 512 * (i + 1)) for i in range(4)], "psSm")
        pOr = self.prot([(2048, 2560), (2560, 3072)], "pOm")
        scale = 96 ** -0.5
        for lb in range(NB):
            tl = [lb] + list(range(2 + 16 * lb, 18 + 16 * lb))
            for hp in range(8):
                opair, opb = opr.next()
                for par in range(2):
                    h = 2 * hp + par
                    kT, kTb = ktr.next()
                    qT, qTb = qtr.next()
                    vh, vhb = vhr.next()
                    kb.dma("sp", kT[0:64, 0:C], self.KT[h, :, ctx_tok(lb):ctx_tok(lb) + C],
                           reads=[self.KTb[t] for t in tl], writes=[kTb])
                    kb.dma("sp", kT[0:64, C:C + S], self.KT[h, :, lat_tok(lb):lat_tok(lb) + S],
                           reads=[self.KTb[t] for t in tl], writes=[kTb])
                    kb.dma("sp", kT[64:96, 0:C], self.KR[:, ctx_tok(lb):ctx_tok(lb) + C],
                           reads=[self.KRb[t] for t in tl], writes=[kTb])
                    kb.dma("sp", kT[64:96, C:C + S], self.KR[:, lat_tok(lb):lat_tok(lb) + S],
                           reads=[self.KRb[t] for t in tl], writes=[kTb])
                    kb.dma("sp", qT, self.QM[h, :, lat_tok(lb):lat_tok(lb) + S],
                           reads=[self.QMb[t] for t in tl[1:]], writes=[qTb])
                    kb.dma("sp", vh[:, 0:2, :],
                           self.VA[ctx_tok(lb):ctx_tok(lb) + C, h * 128:(h + 1) * 128].rearrange("(k p) e -> p k e", p=128),
                           reads=[self.VAb[t] for t in tl], writes=[vhb])
                    for k8 in range(4):
                        r0 = lat_tok(lb) + k8 * 1024
                        kb.dma("sp", vh[:, 2 + 8 * k8:10 + 8 * k8, :],
                               self.VA[r0:r0 + 1024, h * 128:(h + 1) * 128].rearrange("(k p) e -> p k e", p=128),
                               reads=[self.VAb[t] for t in tl], writes=[vhb])
                    for qt in range(8):
                        q0 = qt * 512
                        oap, obuf = pOr.next()
                        sl = {}

                        def emit_S(i):
                            pap, pb = psS.next()
                            kb.mm(pap, pb, [(kT[:, i * 128:(i + 1) * 128], qT[:, q0:q0 + 512])], reads=[kTb, qTb])
                            sl[i] = (pap, pb)

                        emit_S(0)
                        emit_S(1)
                        emit_S(2)
                        for i in range(NK):
                            pap, pb = sl.pop(i)
                            pt, ptb = prot.next()
                            kb.op("act", lambda e, pt=pt, pap=pap: e.activation(out=pt, in_=pap, func=AF.Exp, scale=scale),
                                  reads=[pb], writes=[ptb])
                            if i + 3 < NK:
                                emit_S(i + 3)
                            kb.op("pe", lambda e, pt=pt, i=i, oap=oap: e.matmul(
                                oap, lhsT=vh[:, i, :], rhs=pt, start=(i == 0), stop=(i == NK - 1)),
                                reads=[ptb, vhb], writes=[obuf], inc=True)
                        rs, rsb = rsr.next()
                        olo, ohi = (0, 64) if par == 0 else (64, 128)
                        slo, shi = (64, 128) if par == 0 else (0, 64)
                        kb.op("dve", lambda e, rs=rs, oap=oap: e.reciprocal(out=rs[olo:ohi, :], in_=oap[slo:shi, :]),
                              reads=[obuf], writes=[rsb])
                        kb.op("dve", lambda e, rs=rs, oap=oap: e.tensor_tensor(
                            out=opair[olo:ohi, q0:q0 + 512], in0=oap[olo:ohi, :], in1=rs[olo:ohi, :], op=ALU.mult),
                            reads=[obuf, rsb], writes=[opb])
                kb.dma("pool", self.MT[hp, :, lat_tok(lb):lat_tok(lb) + S], opair, reads=[opb],
                       writes=[self.MTb[t] for t in tl[1:]])
        kb.barrier()
        self.release(mk)


def _chunkT(v):
    v = np.asarray(v, np.float32)
    lead = v.shape[:-1]
    n = v.shape[-1] // 128
    a = v.reshape(lead + (n, 128))
    a = np.moveaxis(a, -1, 0)
    return np.ascontiguousarray(a)


def make_in_maps(inputs, consts, n_cores=8):
    f = lambda k: np.ascontiguousarray(np.asarray(inputs[k], np.float32))
    shared = {}
    for k in ("mod_w", "mlp_w1", "mlp_w2", "fourier_w", "diff_w_qkv", "diff_w_o", "mla_w_down", "mla_w_uq",
              "mla_w_ukv", "mla_w_o"):
        shared[k] = f(k)
    shared["mod_bT"] = _chunkT(f("mod_b"))
    shared["gmixT"] = _chunkT(f("norm_mix_g"))
    shared["gmlpT"] = _chunkT(f("norm_mlp_g"))
    shared["gfinT"] = _chunkT(f("final_g"))
    shared["fbT"] = _chunkT(f("fourier_b"))
    lam = np.stack([f("diff_lambda_q1")[0], f("diff_lambda_k1")[0], f("diff_lambda_q2")[0], f("diff_lambda_k2")[0]], 0)
    shared["dlam"] = np.ascontiguousarray(np.broadcast_to(lam[None], (128, 4, 64))).astype(np.float32)
    shared["dsub"] = np.ascontiguousarray(f("diff_subln_g")[0].reshape(128, 1))
    shared["mgq"] = _chunkT(f("mla_q_norm_g")[0])
    shared["mgkv"] = _chunkT(f("mla_kv_norm_g")[0])
    shared.update(consts)
    x = f("x")
    ctx = f("ctx")
    cvec = f("c")
    cc = f("c_ctx")
    maps = []
    for i in range(n_cores):
        m = dict(shared)
        m["x"] = np.ascontiguousarray(x[NB * i:NB * i + NB])
        m["ctx"] = np.ascontiguousarray(ctx[NB * i:NB * i + NB])
        cond = np.stack([cvec[NB * i], cvec[NB * i + 1], cc], 0)
        m["condT"] = np.ascontiguousarray(cond.reshape(3, 8, 128).transpose(2, 1, 0))
        maps.append(m)
    return maps


_CACHE = {}


def kernel(**inputs):
    if "consts" not in _CACHE:
        _CACHE["consts"] = make_consts()
    prog = Prog(n_layers=DEPTH, debug=False)
    nc = prog.build()
    maps = make_in_maps(inputs, _CACHE["consts"], 8)
    maps = [{k: v for k, v in m.items() if k in prog.din} for m in maps]
    res = run_bass_kernel_spmd(nc, maps, core_ids=list(range(8)))
    outs = [np.asarray(r["out"], np.float32).reshape(NB, S, D) for r in res.results]
    return np.concatenate(outs, axis=0)
```

```python
import math
import numpy as np
import ml_dtypes
import concourse.bass as bass
import concourse.mybir as mybir
from concourse.bass_utils import run_bass_kernel_spmd

F32 = mybir.dt.float32
BF16 = mybir.dt.bfloat16
ALU = mybir.AluOpType
AF = mybir.ActivationFunctionType
AX = mybir.AxisListType

D = 1024
S = 4096
C = 256
NB = 2
TOK = NB * (S + C)
T = 256
NT = TOK // T
DEPTH = 4
EPS = 1e-6
SAME_ENG_SYNC = True


def tinfo(t):
    if t < 2:
        return dict(ctx=True, lb=t, pos=0, cond=2, tok0=t * T)
    lb = (t - 2) // 16
    j = (t - 2) % 16
    return dict(ctx=False, lb=lb, pos=j * T, cond=lb, tok0=t * T)


def ctx_tok(lb):
    return lb * C


def lat_tok(lb):
    return 2 * C + lb * S


class Buf:
    __slots__ = ("w", "r", "name")

    def __init__(self, name=""):
        self.w = {}
        self.r = {}
        self.name = name


class KB:
    def __init__(self, nc):
        self.nc = nc
        self.sems = []
        self.E = {"pe": nc.tensor, "act": nc.scalar, "dve": nc.vector, "pool": nc.gpsimd, "sp": nc.sync}
        self.csem = {}
        self.ccnt = {}
        for e in ("pe", "act", "dve", "pool"):
            self.csem[e] = self.new_sem("c_" + e)
            self.ccnt[e] = 0
        self.rings = {}
        for q, n in (("sp", 28), ("pool", 16), ("act", 4)):
            self.rings[q] = dict(sems=[self.new_sem("d_%s%d" % (q, i)) for i in range(n)], cnt=[0] * n, nxt=0)
        self.waited = {e: {} for e in self.E}
        self.pend = {e: [] for e in self.csem}
        self.n_ins = 0

    def new_sem(self, name):
        h = self.nc.semaphore(name).__enter__()
        self.sems.append(h)
        return len(self.sems) - 1

    @staticmethod
    def _merge(d, s):
        for k, v in s.items():
            if v > d.get(k, 0):
                d[k] = v

    @staticmethod
    def _flat(bs):
        out = []
        for b in bs:
            if isinstance(b, (list, tuple)):
                out.extend(KB._flat(b))
            else:
                out.append(b)
        return out

    def _deps(self, reads, writes):
        reads = self._flat(reads)
        writes = self._flat(writes)
        d = {}
        for b in reads:
            self._merge(d, b.w)
        for b in writes:
            self._merge(d, b.w)
            self._merge(d, b.r)
        return d

    def _wait(self, e, d):
        own = self.csem.get(e)
        wd = self.waited[e]
        for s, v in d.items():
            if v <= wd.get(s, 0):
                continue
            if s == own and (e == "pe" or not SAME_ENG_SYNC):
                continue
            self.E[e].wait_ge(self.sems[s], v)
            self.n_ins += 1
            wd[s] = v

    def _finish(self, tok, reads, writes):
        reads = self._flat(reads)
        writes = self._flat(writes)
        s, v = tok
        for b in reads:
            if v > b.r.get(s, 0):
                b.r[s] = v
        for b in writes:
            if v > b.w.get(s, 0):
                b.w[s] = v

    def op(self, e, fn, reads=(), writes=(), inc=True):
        self._wait(e, self._deps(reads, writes))
        ins = fn(self.E[e])
        self.n_ins += 1
        self.pend[e].append((reads, writes))
        if inc:
            self.ccnt[e] += 1
            ins.then_inc(self.sems[self.csem[e]], 1)
            tok = (self.csem[e], self.ccnt[e])
            for r, w in self.pend[e]:
                self._finish(tok, r, w)
            self.pend[e] = []

    def dma(self, q, out, in_, reads=(), writes=()):
        R = self.rings[q]
        j = R["nxt"]
        R["nxt"] = (j + 1) % len(R["sems"])
        s = R["sems"][j]
        d = self._deps(reads, writes)
        if R["cnt"][j] > d.get(s, 0):
            d[s] = R["cnt"][j]
        self._wait(q, d)
        R["cnt"][j] += 16
        self.E[q].dma_start(out=out, in_=in_).then_inc(self.sems[s], 16)
        self.n_ins += 1
        self._finish((s, R["cnt"][j]), reads, writes)

    def barrier(self):
        for e in self.pend:
            assert not self.pend[e], "pending un-inc'd ops on " + e
        d = {}
        for e in self.csem:
            if self.ccnt[e] > 0:
                d[self.csem[e]] = self.ccnt[e]
        for R in self.rings.values():
            for s, c in zip(R["sems"], R["cnt"]):
                if c:
                    d[s] = c
        for e in self.E:
            self._wait(e, dict(d))

    def mm(self, out_ap, out_buf, pairs, reads):
        n = len(pairs)
        for i, (l, r) in enumerate(pairs):
            self.op("pe",
                    lambda e, l=l, r=r, i=i: e.matmul(out_ap, lhsT=l, rhs=r, start=(i == 0), stop=(i == n - 1)),
                    reads=reads if i == 0 else (), writes=[out_buf] if i == 0 else (), inc=(i == n - 1))


class Rot:
    def __init__(self, aps, name, bufs=None):
        self.aps = aps
        self.bufs = bufs if bufs is not None else [Buf("%s%d" % (name, i)) for i in range(len(aps))]
        self.i = 0

    def next(self):
        j = self.i
        self.i = (j + 1) % len(self.aps)
        return self.aps[j], self.bufs[j]


def _bf(a):
    return np.ascontiguousarray(np.asarray(a, dtype=np.float32).astype(ml_dtypes.bfloat16))


def make_consts():
    cst = {}
    cst["ident"] = np.eye(128, dtype=np.float32)
    t = np.arange(S)
    row = (t // 64).astype(np.float64)
    col = (t % 64).astype(np.float64)

    def rope_tab(dim, dvals):
        nf = dim // 4
        inv = 10000.0 ** (-np.arange(nf, dtype=np.float64) / nf)
        cos = np.zeros((len(dvals), S))
        sin = np.zeros((len(dvals), S))
        for i, d in enumerate(dvals):
            axis = d // (dim // 2)
            pair = (d % (dim // 2)) // nf
            f = d % nf
            ang = (row if axis == 0 else col) * inv[f]
            ang = ang.astype(np.float32).astype(np.float64)
            cos[i] = np.cos(ang)
            sin[i] = np.sin(ang) * (-1.0 if pair == 0 else 1.0)
        return cos, sin

    c64, s64 = rope_tab(64, list(range(64)))
    cst["cosD"] = np.concatenate([c64, c64], 0).astype(np.float32)
    cst["sinD"] = np.concatenate([s64, s64], 0).astype(np.float32)
    P = np.zeros((128, 128), np.float32)
    for m in range(128):
        P[(m // 64) * 64 + ((m % 64) ^ 16), m] = 1.0
    cst["permD"] = _bf(P)
    c32, s32 = rope_tab(32, list(range(32)))
    cst["cosM"] = np.concatenate([np.ones((64, S)), c32], 0).astype(np.float32)
    cst["sinM"] = np.concatenate([np.zeros((64, S)), s32], 0).astype(np.float32)
    cst["cosK"] = c32.astype(np.float32)
    cst["sinK"] = s32.astype(np.float32)
    P = np.zeros((96, 96), np.float32)
    for m in range(64, 96):
        P[64 + ((m - 64) ^ 8), m] = 1.0
    cst["permM"] = _bf(P)
    P = np.zeros((32, 32), np.float32)
    for m in range(32):
        P[m ^ 8, m] = 1.0
    cst["permK"] = _bf(P)
    c = np.arange(256)
    ang = 2 * np.pi * np.outer(c, c) / 256.0
    cst["C256r"] = np.ascontiguousarray(np.cos(ang).reshape(2, 128, 256).transpose(1, 0, 2)).astype(np.float32)
    cst["S256r"] = np.ascontiguousarray(np.sin(ang).reshape(2, 128, 256).transpose(1, 0, 2)).astype(np.float32)
    cst["C256s"] = _bf((np.cos(ang) / 256.0).reshape(2, 128, 256).transpose(1, 0, 2))
    cst["nS256s"] = _bf((-np.sin(ang) / 256.0).reshape(2, 128, 256).transpose(1, 0, 2))
    TA = np.zeros((128, 32, 3, 128), np.float64)
    bb = np.arange(32)
    for q in range(32):
        for al in range(4):
            a = 4 * q + al
            phi = 2 * np.pi * (np.outer(bb, bb) / 32.0 + a * bb[None, :] / 4096.0)
            sl = slice(32 * al, 32 * al + 32)
            TA[sl, q, 0, sl] = np.cos(phi)
            TA[sl, q, 1, sl] = np.sin(phi)
            TA[sl, q, 2, sl] = -np.sin(phi)
    cst["TA"] = _bf(TA)
    a = np.arange(128)
    psi = 2 * np.pi * np.outer(a, a) / 128.0
    TB = np.stack([np.cos(psi) / 1024.0, -np.sin(psi) / 1024.0], 1)
    cst["TB"] = _bf(TB)
    return cst


class Prog:
    def __init__(self, n_layers=DEPTH, debug=False, max_phase=99, noinput=False):
        self.noinput = noinput
        self.max_phase = max_phase
        self.n_layers = n_layers
        self.debug = debug
        nc = bass.Bass("TRN2", target_bir_lowering=False)
        self.nc = nc
        self.kb = KB(nc)
        self.din = {}
        self.stack = []

    def ph(self, fn, *a):
        self._nph += 1
        if self._nph <= self.max_phase:
            fn(*a)

    def inp(self, name, shape, dt=F32):
        t = self.nc.dram_tensor(name, list(shape), dt, kind="Internal" if self.noinput else "ExternalInput").ap()
        self.din[name] = t
        return t

    def scratch(self, name, shape, dt, out=False):
        return self.nc.dram_tensor(name, list(shape), dt, kind="ExternalOutput" if (out and not self.noinput) else "Internal").ap()

    def sb(self, name, shape, dt):
        self._uid = getattr(self, "_uid", 0) + 1
        cm = self.nc.sbuf_tensor("s%d_%s" % (self._uid, name), list(shape), dt)
        h = cm.__enter__()
        self.stack.append(cm)
        return h

    def mark(self):
        return len(self.stack)

    def release(self, mark):
        while len(self.stack) > mark:
            self.stack.pop().__exit__(None, None, None)

    def build(self):
        nc, kb = self.nc, self.kb
        L = self.n_layers
        x_in = self.inp("x", [NB, S, D])
        ctx_in = self.inp("ctx", [NB, C, D])
        condT = self.inp("condT", [128, 8, 3])
        mod_w = self.inp("mod_w", [DEPTH, D, 6 * D])
        mod_bT = self.inp("mod_bT", [128, DEPTH, 48])
        gmixT = self.inp("gmixT", [128, DEPTH, 8])
        gmlpT = self.inp("gmlpT", [128, DEPTH, 8])
        gfinT = self.inp("gfinT", [128, 8])
        mlp_w1 = self.inp("mlp_w1", [DEPTH, D, 4 * D])
        mlp_w2 = self.inp("mlp_w2", [DEPTH, 4 * D, D])
        fw = self.inp("fourier_w", [2, D, D])
        fbT = self.inp("fbT", [128, 2, 8])
        dqkv = self.inp("diff_w_qkv", [1, D, 3 * D])
        dlam = self.inp("dlam", [128, 4, 64])
        dsub = self.inp("dsub", [128, 1])
        dwo = self.inp("diff_w_o", [1, D, D])
        mdown = self.inp("mla_w_down", [1, D, 416])
        mgq = self.inp("mgq", [128, 2])
        mgkv = self.inp("mgkv", [128, 1])
        muq = self.inp("mla_w_uq", [1, 256, 1536])
        mukv = self.inp("mla_w_ukv", [1, 128, 2048])
        mwo = self.inp("mla_w_o", [1, D, D])
        cs = {}
        for k, (shape, dt) in dict(
            ident=([128, 128], F32), cosD=([128, S], F32), sinD=([128, S], F32), permD=([128, 128], BF16),
            cosM=([96, S], F32), sinM=([96, S], F32), cosK=([32, S], F32), sinK=([32, S], F32),
            permM=([96, 96], BF16), permK=([32, 32], BF16), C256r=([128, 2, 256], F32), S256r=([128, 2, 256], F32),
            C256s=([128, 2, 256], BF16), nS256s=([128, 2, 256], BF16), TA=([128, 32, 3, 128], BF16),
            TB=([128, 2, 128], BF16)).items():
            cs[k] = self.inp(k, shape, dt)
        self.cs = cs
        out = self.nc.dram_tensor("out", [NB, S, D], F32, kind="ExternalOutput").ap()
        dbg = self.debug
        self.XT = self.scratch("XT", [8, 128, TOK], F32, out=dbg)
        self.XTb = [Buf("XT%d" % t) for t in range(NT)]
        self.MT = self.scratch("MT", [8, 128, TOK], BF16, out=dbg)
        self.MTb = [Buf("MT%d" % t) for t in range(NT)]
        self.ZS = self.scratch("ZS", [TOK, 2048], BF16)
        self.ZSb = [Buf() for t in range(NT)]
        self.VSF = self.scratch("VSF", [NB, S, 2048], BF16)
        self.VSFb = [Buf() for _ in range(NB)]
        self.QK = self.scratch("QK", [16, 128, TOK], BF16)
        self.QKb = [Buf() for t in range(NT)]
        self.VS = self.scratch("VS", [TOK, 1024], BF16)
        self.VSb = [Buf() for t in range(NT)]
        self.KT = self.scratch("KT", [16, 64, TOK], BF16)
        self.KTb = [Buf() for t in range(NT)]
        self.KR = self.scratch("KR", [32, TOK], BF16)
        self.KRb = [Buf() for t in range(NT)]
        self.QM = self.scratch("QM", [16, 96, TOK], BF16)
        self.QMb = [Buf() for t in range(NT)]
        self.VA = self.scratch("VA", [TOK, 2048], BF16)
        self.VAb = [Buf() for t in range(NT)]
        if dbg:
            self.XS = self.scratch("XS", [DEPTH, 8, 128, 4 * T], F32, out=True)
            self.XSb = Buf("XS")
        self.x_in, self.ctx_in, self.out = x_in, ctx_in, out
        self.outb = Buf("out")

        self.ps = nc.psum_tensor("ps", [128, 4096], F32).__enter__()
        self.psh = [Buf("psb%d" % i) for i in range(8)]

        self.cb = Buf("consts")
        self.ones = self.sb("ones", [128, 128], BF16)
        kb.op("dve", lambda e: e.memset(self.ones[:], 1.0), writes=[self.cb])
        self.ident = self.sb("ident", [128, 128], F32)
        kb.dma("sp", self.ident[:], cs["ident"], writes=[self.cb])
        self.modT = self.sb("modT", [128, DEPTH, 48, 3], F32)
        self.gsA = self.sb("gsA", [128, DEPTH, 8, 3], F32)
        self.gsM = self.sb("gsM", [128, DEPTH, 8, 3], F32)
        self.gb = self.sb("gb", [128, DEPTH, 8, 3], F32)
        self.gfin = self.sb("gfin", [128, 8], F32)
        kb.dma("sp", self.gfin[:], gfinT, writes=[self.cb])
        self.modb = Buf("mod")

        self.phase_mod(condT, mod_w, mod_bT, gmixT, gmlpT, fbT)
        kb.barrier()
        if self.debug:
            dm = self.scratch("dbgmod", [128, L * 48 * 3], F32, out=True)
            kb.dma("sp", dm, self.modT[:, 0:L].rearrange("p l o s -> p (l o s)"), reads=[self.modb])
            kb.barrier()
        self._nph = 0
        for l in range(L):
            kind = l % 3
            last = (l == DEPTH - 1)
            if kind == 0:
                self.ph(self.fourier_front, l, fw[l // 3])
                kb.barrier()
                self.ph(self.fourier_mid, l)
                kb.barrier()
                self.ph(self.tail, l, None, last)
            elif kind == 1:
                self.ph(self.diff_front, l, dqkv[0])
                kb.barrier()
                wmk = self.mark()
                pre = self.alloc_mlp_weights(l)
                self.ph(self.diff_mid, l, dlam, dsub)
                self.bg_flush()
                kb.barrier()
                self.ph(self.tail, l, dwo[0], last, pre)
                self.release(wmk)
            else:
                self.ph(self.mla_front, l, mdown[0], mgq, mgkv, muq[0], mukv[0])
                kb.barrier()
                wmk = self.mark()
                pre = self.alloc_mlp_weights(l)
                self.ph(self.mla_mid, l)
                self.bg_flush()
                kb.barrier()
                self.ph(self.tail, l, mwo[0], last, pre)
                self.release(wmk)
            kb.barrier()
        kb.barrier()
        return nc

    def phase_mod(self, condT, mod_w, mod_bT, gmixT, gmlpT, fbT):
        kb = self.kb
        mk = self.mark()
        cond = self.sb("cond", [128, 8, 3], F32)
        silu = self.sb("silu", [128, 8, 3], F32)
        mb = self.sb("mb", [128, DEPTH, 48], F32)
        gmix = self.sb("gmix", [128, DEPTH, 8], F32)
        gmlp = self.sb("gmlp", [128, DEPTH, 8], F32)
        fb = self.sb("fb", [128, 2, 8], F32)
        b0 = Buf("m0")
        kb.dma("sp", cond[:], condT, writes=[b0])
        kb.dma("sp", mb[:], mod_bT, writes=[b0])
        kb.dma("sp", gmix[:], gmixT, writes=[b0])
        kb.dma("sp", gmlp[:], gmlpT, writes=[b0])
        kb.dma("sp", fb[:], fbT, writes=[b0])
        bs = Buf("silu")
        kb.op("act", lambda e: e.activation(out=silu[:], in_=cond[:], func=AF.Silu), reads=[b0], writes=[bs])
        wst = [self.sb("mwst%d" % i, [128, 8, 1024], F32) for i in range(2)]
        wrot = Rot([w[:] for w in wst], "mwst")
        psb = self.preg(0, 192)[1]
        for l in range(self.n_layers):
            for g in range(6):
                wap, wbuf = wrot.next()
                src = mod_w[l].rearrange("(kc p) n -> p kc n", p=128)[:, :, g * 1024:(g + 1) * 1024]
                kb.dma("sp" if g % 2 == 0 else "pool", wap, src, writes=[wbuf])
                for o in range(8):
                    oc = g * 8 + o
                    pairs = [(wap[:, k, o * 128:(o + 1) * 128], silu[:, k, :]) for k in range(8)]
                    kb.mm(self.ps[:, oc * 4:oc * 4 + 3], psb, pairs, reads=[wbuf, bs])
            psv = self.ps[:, 0:192].rearrange("p (o s) -> p o s", s=4)
            for s_ in range(3):
                kb.op("dve", lambda e, s_=s_, l=l: e.tensor_tensor(out=self.modT[:, l, :, s_], in0=psv[:, :, s_],
                                                                   in1=mb[:, l, :], op=ALU.add),
                      reads=[psb, b0], writes=[self.modb])
            for s_ in range(3):
                kb.op("dve", lambda e, s_=s_, l=l: e.scalar_tensor_tensor(
                    out=self.gsA[:, l, :, s_], in0=self.modT[:, l, 8:16, s_], scalar=1.0, in1=gmix[:, l, :],
                    op0=ALU.add, op1=ALU.mult), reads=[self.modb, b0], writes=[self.modb])
                kb.op("dve", lambda e, s_=s_, l=l: e.scalar_tensor_tensor(
                    out=self.gsM[:, l, :, s_], in0=self.modT[:, l, 32:40, s_], scalar=1.0, in1=gmlp[:, l, :],
                    op0=ALU.add, op1=ALU.mult), reads=[self.modb, b0], writes=[self.modb])
                if l % 3 == 0:
                    kb.op("dve", lambda e, s_=s_, l=l: e.tensor_tensor(
                        out=self.gb[:, l, :, s_], in0=self.modT[:, l, 16:24, s_], in1=fb[:, l // 3, :], op=ALU.mult),
                        reads=[self.modb, b0], writes=[self.modb])
        kb.barrier()
        self.release(mk)

    def bg_step(self):
        if getattr(self, "bg", None):
            self.bg.pop(0)()

    def bg_flush(self):
        while getattr(self, "bg", None):
            self.bg.pop(0)()

    def alloc_mlp_weights(self, l):
        w1 = self.sb("w1", [128, 8, 4 * D], BF16)
        w2 = self.sb("w2", [128, 32, D], BF16)
        w1b, w2b = Buf("w1"), Buf("w2")
        self.bg = []
        for k in range(8):
            self.bg.append(lambda k=k: self.load_w_bf16(w1[:, k, :], self.din["mlp_w1"][l, k * 128:(k + 1) * 128, :], w1b))
        for f4 in range(8):
            self.bg.append(lambda f4=f4: self.load_w_bf16(
                w2[:, f4 * 4:(f4 + 1) * 4, :],
                self.din["mlp_w2"][l, f4 * 512:(f4 + 1) * 512, :].rearrange("(f p) n -> p f n", p=128), w2b))
        return (w1, w2, w1b, w2b)

    def shA(self, l, k, s):
        return self.modT[:, l, k, s:s + 1]

    def gateA(self, l, k, s):
        return self.modT[:, l, 16 + k, s:s + 1]

    def shM(self, l, k, s):
        return self.modT[:, l, 24 + k, s:s + 1]

    def gateM(self, l, k, s):
        return self.modT[:, l, 40 + k, s:s + 1]

    def alloc_common(self, n_xt=2):
        c = {}
        xts = [self.sb("xt%d" % i, [128, 8, T], F32) for i in range(n_xt)]
        c["xt"] = Rot([x[:] for x in xts], "xt")
        self._sq = self.sb("sq", [128, 8, T], BF16)
        c["sq"] = (self._sq[:], Buf("sq"))
        self._st = self.sb("stt", [128, 3, T], F32)
        c["lnv"] = (self._st[:, 0, :], Buf("lnv"))
        c["rstd"] = (self._st[:, 1, :], Buf("rstd"))
        tm = self.sb("tmpm", [128, 2, T], F32)
        c["tmp"] = Rot([tm[:, i, :] for i in range(2)], "tmp")
        self._hb = self.sb("hb", [128, 8, T], BF16)
        c["hb"] = (self._hb[:], Buf("hb"))
        return c

    def load_xt(self, c, l, t, from_input):
        kb = self.kb
        xt, xb = c["xt"].next()
        ti = tinfo(t)
        if not from_input:
            kb.dma("sp", xt, self.XT[:, :, ti["tok0"]:ti["tok0"] + T].rearrange("m p t -> p m t"),
                   reads=[self.XTb[t]], writes=[xb])
            return xt, xb
        xin, xinb = c["xin"].next()
        if ti["ctx"]:
            src = self.ctx_in[ti["lb"]].rearrange("(u p) d -> p u d", p=128)
        else:
            src = self.x_in[ti["lb"], ti["pos"]:ti["pos"] + T, :].rearrange("(u p) d -> p u d", p=128)
        kb.dma("sp", xin, src, writes=[xinb])
        for half in range(2):
            pap, pb = c["ptr"].next()
            pv = pap.rearrange("p (m t) -> p m t", t=T)
            for mm_ in range(4):
                m = half * 4 + mm_
                for u in range(2):
                    lastone = (mm_ == 3 and u == 1)
                    kb.op("pe", lambda e, m=m, u=u, mm_=mm_: e.transpose(
                        out=pv[:, mm_, u * 128:(u + 1) * 128], in_=xin[:, u, m * 128:(m + 1) * 128],
                        identity=self.ident[:]),
                        reads=[xinb, self.cb] if (mm_ == 0 and u == 0) else (),
                        writes=[pb] if (mm_ == 0 and u == 0) else (), inc=lastone)
            eng = "dve" if half == 0 else "act"
            if eng == "dve":
                kb.op("dve", lambda e, half=half: e.tensor_copy(out=xt[:, half * 4:half * 4 + 4, :], in_=pv),
                      reads=[pb], writes=[xb])
            else:
                kb.op("act", lambda e, half=half: e.copy(out=xt[:, half * 4:half * 4 + 4, :], in_=pv),
                      reads=[pb], writes=[xb])
        return xt, xb

    def stats(self, c, src_ap, src_buf, nchunk, pstat, inv_n):
        kb = self.kb
        sq, sqb = c["sq"]
        kb.op("act", lambda e: e.activation(out=sq[:, 0:nchunk, :], in_=src_ap, func=AF.Square),
              reads=[src_buf], writes=[sqb])
        pap, pb = pstat
        kb.mm(pap, pb, [(self.ones[:], sq[:, k, :]) for k in range(nchunk)], reads=[sqb, self.cb])
        lnv, lb_ = c["lnv"]
        rstd, rb = c["rstd"]
        kb.op("act", lambda e: e.activation(out=lnv, in_=pap, func=AF.Ln, bias=EPS, scale=inv_n),
              reads=[pb], writes=[lb_])
        kb.op("act", lambda e: e.activation(out=rstd, in_=lnv, func=AF.Exp, scale=-0.5),
              reads=[lb_], writes=[rb])
        return rstd, rb

    def modulate(self, c, xt, xb, l, cond, which, pstat):
        kb = self.kb
        rstd, rb = self.stats(c, xt, xb, 8, pstat, 1.0 / D)
        hb, hbb = c["hb"]
        gs = self.gsA if which == "a" else self.gsM
        for k in range(8):
            tmp, tb = c["tmp"].next()
            kb.op("dve", lambda e, k=k, tmp=tmp: e.tensor_tensor(out=tmp, in0=xt[:, k, :], in1=rstd, op=ALU.mult),
                  reads=[xb, rb], writes=[tb])
            sh = self.shA(l, k, cond) if which == "a" else self.shM(l, k, cond)
            kb.op("act", lambda e, k=k, tmp=tmp, sh=sh: e.activation(
                out=hb[:, k, :], in_=tmp, func=AF.Identity, bias=sh, scale=gs[:, l, k, cond:cond + 1]),
                reads=[tb, self.modb], writes=[hbb])
        return hb, hbb

    def load_w_bf16(self, dst_ap, src_ap, buf, q="pool"):
        self.kb.dma("pool", dst_ap, src_ap, writes=[buf])

    def preg(self, lo, hi):
        return (self.ps[:, lo:hi], [self.psh[i] for i in range(lo // 512, (hi - 1) // 512 + 1)])

    def prot(self, regs, name):
        aps, bufs = [], []
        for lo, hi in regs:
            a, b = self.preg(lo, hi)
            aps.append(a)
            bufs.append(b)
        return Rot(aps, name, bufs)

    def psum_slots(self, bank0, nslots, width, name):
        regs = [((bank0 + i) * 512, (bank0 + i) * 512 + width) for i in range(nslots)]
        assert bank0 + nslots <= 8
        return self.prot(regs, name)

    def fourier_front(self, l, w):
        kb = self.kb
        mk = self.mark()
        first = (l == 0)
        c = self.alloc_common()
        if first:
            xi = [self.sb("xin%d" % i, [128, 2, D], F32) for i in range(2)]
            c["xin"] = Rot([x[:] for x in xi], "xin")
            c["ptr"] = self.prot([(2048, 3072), (3072, 4096)], "ptr")
        pstat = self.preg(0, T)
        pz = self.prot([(512, 1024), (1024, 1536), (1536, 2048)], "pz")
        wcs = self.sb("wcs", [128, 8, 2048], BF16)
        wcsb = Buf("wcs")
        mk2 = self.mark()
        wf = self.sb("wf", [128, 8, D], F32)
        cr = self.sb("c256r", [128, 2, 2, 256], F32)
        wfb = Buf("wf")
        kb.dma("sp", wf[:], w.rearrange("(kc p) n -> p kc n", p=128), writes=[wfb])
        kb.dma("sp", cr[:, 0], self.cs["C256r"], writes=[wfb])
        kb.dma("sp", cr[:, 1], self.cs["S256r"], writes=[wfb])
        i = 0
        for tb_ in range(2):
            for g in range(4):
                for cc in range(2):
                    for nh in range(2):
                        pap, pb = pz.next()
                        pairs = [(cr[:, tb_, kc, cc * 128:(cc + 1) * 128], wf[:, g * 2 + kc, nh * 512:(nh + 1) * 512])
                                 for kc in range(2)]
                        kb.mm(pap, pb, pairs, reads=[wfb])
                        dst = wcs[:, g * 2 + cc, tb_ * 1024 + nh * 512: tb_ * 1024 + (nh + 1) * 512]
                        if i % 2 == 0:
                            kb.op("dve", lambda e, dst=dst, pap=pap: e.tensor_copy(out=dst, in_=pap),
                                  reads=[pb], writes=[wcsb])
                        else:
                            kb.op("act", lambda e, dst=dst, pap=pap: e.copy(out=dst, in_=pap),
                                  reads=[pb], writes=[wcsb])
                        i += 1
        kb.barrier()
        self.release(mk2)
        zs = [self.sb("zsb%d" % i, [128, 2, 2048], BF16) for i in range(2)]
        zrot = Rot([z[:] for z in zs], "zsb")
        tiles = list(range(NT)) if l == 0 else list(range(2, NT))
        for t in tiles:
            ti = tinfo(t)
            xt, xb = self.load_xt(c, l, t, first)
            if first:
                kb.dma("pool", self.XT[:, :, ti["tok0"]:ti["tok0"] + T].rearrange("m p t -> p m t"), xt,
                       reads=[xb], writes=[self.XTb[t]])
            hb, hbb = self.modulate(c, xt, xb, l, ti["cond"], "a", pstat)
            zsb, zb = zrot.next()
            i = 0
            for u in range(2):
                for nh in range(4):
                    pap, pb = pz.next()
                    pairs = [(hb[:, k, u * 128:(u + 1) * 128], wcs[:, k, nh * 512:(nh + 1) * 512]) for k in range(8)]
                    kb.mm(pap, pb, pairs, reads=[hbb, wcsb])
                    dst = zsb[:, u, nh * 512:(nh + 1) * 512]
                    if i % 2 == 0:
                        kb.op("dve", lambda e, dst=dst, pap=pap: e.tensor_copy(out=dst, in_=pap),
                              reads=[pb], writes=[zb])
                    else:
                        kb.op("pool", lambda e, dst=dst, pap=pap: None, reads=(), writes=()) if False else \
                            kb.op("act", lambda e, dst=dst, pap=pap: e.copy(out=dst, in_=pap), reads=[pb], writes=[zb])
                    i += 1
            kb.dma("pool", self.ZS[ti["tok0"]:ti["tok0"] + T, :].rearrange("(u p) n -> p u n", p=128), zsb,
                   reads=[zb], writes=[self.ZSb[t]])
        kb.barrier()
        self.release(mk)

    def fourier_mid(self, l):
        kb = self.kb
        mk = self.mark()
        ta = self.sb("ta", [128, 32, 3, 128], BF16)
        tbm = self.sb("tbm", [128, 2, 128], BF16)
        c256 = self.sb("c256s", [128, 2, 2, 256], BF16)
        tabb = Buf("tab")
        kb.dma("sp", ta[:], self.cs["TA"], writes=[tabb])
        kb.dma("sp", tbm[:], self.cs["TB"], writes=[tabb])
        kb.dma("sp", c256[:, 0], self.cs["C256s"], writes=[tabb])
        kb.dma("sp", c256[:, 1], self.cs["nS256s"], writes=[tabb])
        zq_t = [self.sb("zq%d" % i, [128, 2048], BF16) for i in range(2)]
        zq = Rot([z[:] for z in zq_t], "zq")
        vs_t = [self.sb("vsb%d" % i, [128, 2048], BF16) for i in range(2)]
        vsr = Rot([z[:] for z in vs_t], "vsb")
        vb_t = [self.sb("vb%d" % i, [128, 2048], BF16) for i in range(2)]
        vbr = Rot([z[:] for z in vb_t], "vb")
        ysb = self.sb("ysb", [128, 8, S], BF16)
        ysbb = Buf("ysb")
        ysb2 = self.sb("ysb2", [128, 8, S], BF16)
        ysb2b = Buf("ysb2")
        pa = self.prot([(0, 2048), (2048, 4096)], "pa")
        import os
        FM = int(os.environ.get("FM_STOP", "9"))
        for lb in range(NB):
            if FM <= 1:
                break
            lt0 = 2 + 16 * lb
            zsbufs = [self.ZSb[t] for t in range(lt0, lt0 + 16)]
            base = lat_tok(lb)
            for q in range(32):
                zt, zb = zq.next()
                for al in range(4):
                    r0 = base + 4 * q + al
                    kb.dma("sp", zt[32 * al:32 * al + 32, :], self.ZS[r0:r0 + 31 * 128 + 1:128, :], reads=zsbufs, writes=[zb])
                pap, pb = pa.next()
                for o in range(4):
                    nh = o % 2
                    A = zt[:, nh * 512:(nh + 1) * 512]
                    B = zt[:, 1024 + nh * 512:1024 + (nh + 1) * 512]
                    if o < 2:
                        pairs = [(ta[:, q, 0, :], A), (ta[:, q, 2, :], B)]
                    else:
                        pairs = [(ta[:, q, 0, :], B), (ta[:, q, 1, :], A)]
                    for i, (lh, rh) in enumerate(pairs):
                        kb.op("pe", lambda e, lh=lh, rh=rh, i=i, o=o: e.matmul(
                            pap[:, o * 512:(o + 1) * 512], lhsT=lh, rhs=rh, start=(i == 0), stop=(i == 1)),
                            reads=[zb, tabb] if (o == 0 and i == 0) else (),
                            writes=[pb] if (o == 0 and i == 0) else (), inc=(o == 3 and i == 1))
                vt, vb_ = vsr.next()
                kb.op("dve", lambda e, vt=vt, pap=pap: e.tensor_copy(out=vt[:, 0:1024], in_=pap[:, 0:1024]),
                      reads=[pb], writes=[vb_])
                kb.op("act", lambda e, vt=vt, pap=pap: e.copy(out=vt[:, 1024:2048], in_=pap[:, 1024:2048]),
                      reads=[pb], writes=[vb_])
                for al in range(4):
                    a = 4 * q + al
                    kb.dma("pool", self.VSF[lb, a:a + 31 * 128 + 1:128, :], vt[32 * al:32 * al + 32, :],
                           reads=[vb_], writes=[self.VSFb[lb]])
            if FM <= 2:
                continue
            ysv = ysb[:].rearrange("p m (a r) -> p m a r", r=32)
            pbr = self.prot([(0, 1024), (1024, 2048), (2048, 3072), (3072, 4096)], "pbr")
            for bp in range(32):
                vt, vb_ = vbr.next()
                kb.dma("sp", vt, self.VSF[lb, bp * 128:(bp + 1) * 128, :], reads=[self.VSFb[lb]], writes=[vb_])
                pap, pb = pbr.next()
                pv = pap.rearrange("p (m a) -> p m a", a=128)
                for m in range(8):
                    for i in range(2):
                        kb.op("pe", lambda e, m=m, i=i, vt=vt, pv=pv: e.matmul(
                            pv[:, m, :], lhsT=vt[:, i * 1024 + m * 128: i * 1024 + (m + 1) * 128], rhs=tbm[:, i, :],
                            start=(i == 0), stop=(i == 1)),
                            reads=[vb_, tabb] if (m == 0 and i == 0) else (),
                            writes=[pb] if (m == 0 and i == 0) else (), inc=(m == 7 and i == 1))
                if bp % 2 == 0:
                    kb.op("dve", lambda e, bp=bp, pv=pv: e.tensor_copy(out=ysb[:, :, bp * 128:(bp + 1) * 128], in_=pv),
                          reads=[pb], writes=[ysbb])
                else:
                    kb.op("act", lambda e, bp=bp, pv=pv: e.copy(out=ysb[:, :, bp * 128:(bp + 1) * 128], in_=pv),
                          reads=[pb], writes=[ysbb])
            for m in range(8):
                src = ysb[:, m, :].rearrange("p (b a) -> p a b", a=128)
                dst = ysb2[:, m, :].rearrange("p (a b) -> p a b", b=32)
                eng = ("dve", "pool", "act")[m % 3]
                if eng == "act":
                    kb.op("act", lambda e, src=src, dst=dst: e.copy(out=dst, in_=src), reads=[ysbb], writes=[ysb2b])
                else:
                    kb.op(eng, lambda e, src=src, dst=dst: e.tensor_copy(out=dst, in_=src), reads=[ysbb], writes=[ysb2b])
            for m in range(8):
                if FM <= 3:
                    break
                kb.dma("pool", self.MT[m, :, base:base + S], ysb2[:, m, :], reads=[ysb2b],
                       writes=[self.MTb[t] for t in range(lt0, lt0 + 16)])
        if l == 0 and FM >= 5:
            kb.barrier()
            zc_t = self.sb("zc", [128, 2, 2048], BF16)
            yc = self.sb("yc", [128, 8, C], BF16)
            pcr = self.psum_slots(0, 4, C, "pc")
            for lb in range(NB):
                zcb = Buf("zc")
                ycb = Buf("yc")
                t = lb
                kb.dma("sp", zc_t[:], self.ZS[ctx_tok(lb):ctx_tok(lb) + C, :].rearrange("(u p) n -> p u n", p=128),
                       reads=[self.ZSb[t]], writes=[zcb])
                for m in range(8):
                    pap, pb = pcr.next()
                    pairs = []
                    for u in range(2):
                        pairs.append((zc_t[:, u, m * 128:(m + 1) * 128], c256[:, 0, u, :]))
                        pairs.append((zc_t[:, u, 1024 + m * 128:1024 + (m + 1) * 128], c256[:, 1, u, :]))
                    kb.mm(pap, pb, pairs, reads=[zcb, tabb])
                    kb.op("dve", lambda e, m=m, pap=pap: e.tensor_copy(out=yc[:, m, :], in_=pap), reads=[pb], writes=[ycb])
                if int(os.environ.get("FM_SUB", "0")) != 7:
                    kb.dma("pool", self.MT[:, :, ctx_tok(lb):ctx_tok(lb) + C].rearrange("m p t -> p m t"), yc[:],
                           reads=[ycb], writes=[self.MTb[t]])
                kb.barrier()
        kb.barrier()
        self.release(mk)

    def tail(self, l, wo_dram, last, pre=None):
        kb = self.kb
        mk = self.mark()
        c = self.alloc_common()
        fourier = wo_dram is None
        if pre is None:
            w1 = self.sb("w1", [128, 8, 4 * D], BF16)
            w2 = self.sb("w2", [128, 32, D], BF16)
            w1b, w2b = Buf("w1"), Buf("w2")
        else:
            w1, w2, w1b, w2b = pre
        wob = Buf("wo")
        if not fourier:
            wo = self.sb("wo", [128, 8, D], BF16)
            self.load_w_bf16(wo[:], wo_dram.rearrange("(kc p) n -> p kc n", p=128), wob)
        s1 = self.cs
        src1 = None
        if pre is None:
            for k in range(8):
                self.load_w_bf16(w1[:, k, :], self.din["mlp_w1"][l, k * 128:(k + 1) * 128, :], w1b)
            for f4 in range(8):
                self.load_w_bf16(w2[:, f4 * 4:(f4 + 1) * 4, :],
                                 self.din["mlp_w2"][l, f4 * 512:(f4 + 1) * 512, :].rearrange("(f p) n -> p f n", p=128), w2b)
        mt_t = self.sb("mt", [128, 8, T], BF16)
        mtb = Buf("mt")
        a_t = self.sb("a_t", [128, 32, T], BF16)
        ab = Buf("a")
        r_t = self.sb("r_t", [128, 3, T], F32)
        rrot = Rot([r_t[:, i, :] for i in range(3)], "r")
        pstat = self.preg(0, T)
        pstat2 = self.preg(T, 2 * T)
        pop = self.psum_slots(1, 2, T, "pop")
        ph = self.psum_slots(3, 3, T, "ph")
        py = self.psum_slots(6, 2, T, "py")
        if last:
            of_t = self.sb("of", [128, 8, T], F32)
            ofb = Buf("of")
            os_t = [self.sb("osb%d" % i, [128, D], F32) for i in range(2)]
            osr = Rot([o[:] for o in os_t], "osb")
            assert fourier
            ptr = self.prot([(512, 1024), (1024, 1536)], "ptr2")
        tiles = list(range(NT))
        if l >= 2:
            tiles = list(range(2, NT))
        state = {}

        def part1(t):
            ti = tinfo(t)
            cond = ti["cond"]
            xt, xb = self.load_xt(c, l, t, False)
            kb.dma("sp", mt_t[:], self.MT[:, :, ti["tok0"]:ti["tok0"] + T].rearrange("m p t -> p m t"),
                   reads=[self.MTb[t]], writes=[mtb])
            for m in range(8):
                if fourier:
                    kb.op("dve", lambda e, m=m: e.scalar_tensor_tensor(
                        out=xt[:, m, :], in0=mt_t[:, m, :], scalar=self.gateA(l, m, cond), in1=xt[:, m, :],
                        op0=ALU.mult, op1=ALU.add), reads=[mtb, xb, self.modb], writes=[xb])
                    kb.op("pool", lambda e, m=m: e.tensor_scalar(
                        out=xt[:, m, :], in0=xt[:, m, :], scalar1=self.gb[:, l, m, cond:cond + 1], scalar2=None,
                        op0=ALU.add), reads=[xb, self.modb], writes=[xb])
                else:
                    pap, pb = pop.next()
                    kb.mm(pap, pb, [(wo[:, k, m * 128:(m + 1) * 128], mt_t[:, k, :]) for k in range(8)],
                          reads=[mtb, wob])
                    kb.op("dve", lambda e, m=m, pap=pap: e.scalar_tensor_tensor(
                        out=xt[:, m, :], in0=pap, scalar=self.gateA(l, m, cond), in1=xt[:, m, :],
                        op0=ALU.mult, op1=ALU.add), reads=[pb, xb, self.modb], writes=[xb])
            hb, hbb = self.modulate(c, xt, xb, l, cond, "m", pstat)
            state[t] = (xt, xb, hb, hbb)

        def part2(t):
            xt, xb, hb, hbb = state[t]
            for f in range(32):
                pap, pb = ph.next()
                kb.mm(pap, pb, [(w1[:, k, f * 128:(f + 1) * 128], hb[:, k, :]) for k in range(8)], reads=[hbb, w1b])
                r, rb = rrot.next()
                kb.op("act", lambda e, r=r, pap=pap: e.activation(out=r, in_=pap, func=AF.Relu), reads=[pb], writes=[rb])
                eng = "dve" if f % 2 == 0 else "pool"
                kb.op(eng, lambda e, r=r, f=f: e.tensor_tensor(out=a_t[:, f, :], in0=r, in1=r, op=ALU.mult),
                      reads=[rb], writes=[ab])

        def part3(t):
            ti = tinfo(t)
            cond = ti["cond"]
            xt, xb, hb, hbb = state.pop(t)
            for m in range(8):
                pap, pb = py.next()
                kb.mm(pap, pb, [(w2[:, f, m * 128:(m + 1) * 128], a_t[:, f, :]) for f in range(32)], reads=[ab, w2b])
                kb.op("dve", lambda e, m=m, pap=pap: e.scalar_tensor_tensor(
                    out=xt[:, m, :], in0=pap, scalar=self.gateM(l, m, cond), in1=xt[:, m, :],
                    op0=ALU.mult, op1=ALU.add), reads=[pb, xb, self.modb], writes=[xb])
            SAMP = {0: 0, 2: 1, 17: 2, 33: 3}
            if self.debug and t in SAMP:
                kb.dma("pool", self.XS[l][:, :, SAMP[t] * T:(SAMP[t] + 1) * T].rearrange("m p t -> p m t"), xt,
                       reads=[xb], writes=[self.XSb])
            if not last:
                kb.dma("pool", self.XT[:, :, ti["tok0"]:ti["tok0"] + T].rearrange("m p t -> p m t"), xt,
                       reads=[xb], writes=[self.XTb[t]])
                return
            if self.debug:
                kb.dma("pool", self.XT[:, :, ti["tok0"]:ti["tok0"] + T].rearrange("m p t -> p m t"), xt,
                       reads=[xb], writes=[self.XTb[t]])
            rstd, rb = self.stats(c, xt, xb, 8, pstat2, 1.0 / D)
            for m in range(8):
                kb.op("dve", lambda e, m=m: e.scalar_tensor_tensor(
                    out=of_t[:, m, :], in0=xt[:, m, :], scalar=self.gfin[:, m:m + 1], in1=rstd,
                    op0=ALU.mult, op1=ALU.mult), reads=[xb, rb, self.cb], writes=[ofb])
            for u in range(2):
                osb, osbb = osr.next()
                for half in range(2):
                    pap, pb = ptr.next()
                    for mm_ in range(4):
                        m = half * 4 + mm_
                        kb.op("pe", lambda e, m=m, mm_=mm_, u=u, pap=pap: e.transpose(
                            out=pap[:, mm_ * 128:(mm_ + 1) * 128], in_=of_t[:, m, u * 128:(u + 1) * 128],
                            identity=self.ident[:]),
                            reads=[ofb, self.cb] if mm_ == 0 else (), writes=[pb] if mm_ == 0 else (), inc=(mm_ == 3))
                    if half == 0:
                        kb.op("dve", lambda e, pap=pap, osb=osb: e.tensor_copy(out=osb[:, 0:512], in_=pap),
                              reads=[pb], writes=[osbb])
                    else:
                        kb.op("act", lambda e, pap=pap, osb=osb: e.copy(out=osb[:, 512:1024], in_=pap),
                              reads=[pb], writes=[osbb])
                p0 = ti["pos"] + u * 128
                kb.dma("pool", self.out[ti["lb"], p0:p0 + 128, :], osb, reads=[osbb], writes=[self.outb])

        part1(tiles[0])
        for i, t in enumerate(tiles):
            part2(t)
            if i + 1 < len(tiles):
                part1(tiles[i + 1])
            part3(t)
        kb.barrier()
        self.release(mk)

    def diff_front(self, l, wqkv_dram):
        kb = self.kb
        mk = self.mark()
        c = self.alloc_common()
        wq = self.sb("wqkv", [128, 8, 3 * D], BF16)
        wqb = Buf("wqkv")
        for k in range(8):
            self.load_w_bf16(wq[:, k, :], wqkv_dram[k * 128:(k + 1) * 128, :], wqb)
        perm = self.sb("permD", [128, 128], BF16)
        kb.dma("sp", perm[:], self.cs["permD"], writes=[wqb])
        cs_t = [self.sb("csD%d" % i, [128, 2, T], F32) for i in range(2)]
        csr = Rot([x[:] for x in cs_t], "csD")
        qk_t = [self.sb("qk%d" % i, [128, 16, T], BF16) for i in range(2)]
        qkr = Rot([x[:] for x in qk_t], "qk")
        vs_t = [self.sb("vsd%d" % i, [128, 2, D], BF16) for i in range(2)]
        vsr = Rot([x[:] for x in vs_t], "vsd")
        qraw_t = self.sb("qraw", [128, 3, T], BF16)
        qrr = Rot([qraw_t[:, i, :] for i in range(3)], "qraw")
        t1_t = self.sb("t1", [128, 3, T], F32)
        t1r = Rot([t1_t[:, i, :] for i in range(3)], "t1")
        t2_t = self.sb("t2", [128, 3, T], F32)
        t2r = Rot([t2_t[:, i, :] for i in range(3)], "t2")
        pstat = self.preg(0, T)
        pq = self.psum_slots(1, 3, T, "pq")
        pr = self.psum_slots(4, 2, T, "pr")
        pv = self.prot([(3072, 3584), (3584, 4096)], "pv")
        for t in range(NT):
            ti = tinfo(t)
            xt, xb = self.load_xt(c, l, t, False)
            hb, hbb = self.modulate(c, xt, xb, l, ti["cond"], "a", pstat)
            qk, qkb = qkr.next()
            if not ti["ctx"]:
                cst, csb = csr.next()
                kb.dma("sp", cst[:, 0, :], self.cs["cosD"][:, ti["pos"]:ti["pos"] + T], writes=[csb])
                kb.dma("sp", cst[:, 1, :], self.cs["sinD"][:, ti["pos"]:ti["pos"] + T], writes=[csb])
            for j in range(16):
                pap, pb = pq.next()
                kb.mm(pap, pb, [(wq[:, k, j * 128:(j + 1) * 128], hb[:, k, :]) for k in range(8)], reads=[hbb, wqb])
                if ti["ctx"]:
                    kb.op("act", lambda e, j=j, pap=pap: e.copy(out=qk[:, j, :], in_=pap), reads=[pb], writes=[qkb])
                    continue
                qr, qrb = qrr.next()
                kb.op("act", lambda e, qr=qr, pap=pap: e.copy(out=qr, in_=pap), reads=[pb], writes=[qrb])
                pap2, pb2 = pr.next()
                kb.mm(pap2, pb2, [(perm[:], qr)], reads=[qrb, wqb])
                t1, t1b = t1r.next()
                t2, t2b = t2r.next()
                kb.op("pool", lambda e, t1=t1, qr=qr: e.tensor_tensor(out=t1, in0=qr, in1=cst[:, 0, :], op=ALU.mult),
                      reads=[qrb, csb], writes=[t1b])
                kb.op("dve", lambda e, t2=t2, pap2=pap2: e.tensor_tensor(out=t2, in0=pap2, in1=cst[:, 1, :], op=ALU.mult),
                      reads=[pb2, csb], writes=[t2b])
                kb.op("pool", lambda e, t1=t1, t2=t2, j=j: e.tensor_tensor(out=qk[:, j, :], in0=t1, in1=t2, op=ALU.add),
                      reads=[t1b, t2b], writes=[qkb])
            vs, vsb_ = vsr.next()
            i = 0
            for u in range(2):
                for nh in range(2):
                    pap, pb = pv.next()
                    kb.mm(pap, pb, [(hb[:, k, u * 128:(u + 1) * 128], wq[:, k, 2048 + nh * 512:2048 + (nh + 1) * 512])
                                    for k in range(8)], reads=[hbb, wqb])
                    dst = vs[:, u, nh * 512:(nh + 1) * 512]
                    if i % 2 == 0:
                        kb.op("dve", lambda e, dst=dst, pap=pap: e.tensor_copy(out=dst, in_=pap), reads=[pb], writes=[vsb_])
                    else:
                        kb.op("act", lambda e, dst=dst, pap=pap: e.copy(out=dst, in_=pap), reads=[pb], writes=[vsb_])
                    i += 1
            for j4 in range(4):
                kb.dma("pool", self.QK[4 * j4:4 * j4 + 4, :, ti["tok0"]:ti["tok0"] + T].rearrange("j p t -> p j t"),
                       qk[:, 4 * j4:4 * j4 + 4, :], reads=[qkb], writes=[self.QKb[t]])
            kb.dma("pool", self.VS[ti["tok0"]:ti["tok0"] + T, :].rearrange("(u p) n -> p u n", p=128), vs,
                   reads=[vsb_], writes=[self.VSb[t]])
        kb.barrier()
        self.release(mk)

    def diff_mid(self, l, dlam, dsub):
        kb = self.kb
        mk = self.mark()
        lam_init = 0.8 - 0.6 * math.exp(-0.3 * l)
        lm = self.sb("lm", [128, 4, 64], F32)
        lmb = Buf("lm")
        kb.dma("sp", lm[:], dlam, writes=[lmb])
        sub = self.sb("subg", [128, 2], F32)
        kb.dma("sp", sub[:, 0:1], dsub, writes=[lmb])
        pr_ = self.sb("lpr", [128, 2, 64], F32)
        sc_ = self.sb("lsc", [128, 4], F32)
        scb = Buf("lsc")
        for i in range(2):
            kb.op("dve", lambda e, i=i: e.tensor_tensor(out=pr_[:, i, :], in0=lm[:, 2 * i, :], in1=lm[:, 2 * i + 1, :],
                                                        op=ALU.mult), reads=[lmb], writes=[scb])
            kb.op("dve", lambda e, i=i: e.reduce_sum(out=sc_[:, i:i + 1], in_=pr_[:, i, :], axis=AX.X),
                  reads=[scb], writes=[scb])
        kb.op("act", lambda e: e.activation(out=sc_[:, 0:2], in_=sc_[:, 0:2], func=AF.Exp), reads=[scb], writes=[scb])
        kb.op("dve", lambda e: e.tensor_tensor(out=sc_[:, 2:3], in0=sc_[:, 0:1], in1=sc_[:, 1:2], op=ALU.subtract),
              reads=[scb], writes=[scb])
        kb.op("dve", lambda e: e.tensor_scalar(out=sc_[:, 3:4], in0=sc_[:, 2:3], scalar1=lam_init, scalar2=-1.0,
                                               op0=ALU.add, op1=ALU.mult), reads=[scb], writes=[scb])
        kb.op("dve", lambda e: e.tensor_scalar(out=sub[:, 1:2], in0=sub[:, 0:1], scalar1=(1.0 - lam_init), scalar2=None,
                                               op0=ALU.mult), reads=[lmb], writes=[scb])
        nlam = sc_[:, 3:4]
        gsub = sub[:, 1:2]
        NK = 34
        kt_t = [self.sb("kT%d" % i, [128, NK * 128], BF16) for i in range(2)]
        ktr = Rot([x[:] for x in kt_t], "kT")
        qt_t = [self.sb("qT%d" % i, [128, NK * 128], BF16) for i in range(2)]
        qtr = Rot([x[:] for x in qt_t], "qT")
        vh_t = [self.sb("vh%d" % i, [128, NK, 128], BF16) for i in range(2)]
        vhr = Rot([x[:] for x in vh_t], "vh")
        p_t = self.sb("pT", [128, 4, 512], BF16)
        prot = Rot([p_t[:, i, :] for i in range(4)], "pT")
        fin = self.sb("fin", [128, 5, 512], F32)
        finb = [Buf("fin%d" % i) for i in range(5)]
        sqf = self.sb("sqf", [128, 512], BF16)
        sqfb = Buf("sqf")
        ob_t = self.sb("obt", [128, 2, 512], BF16)
        obr = Rot([ob_t[:, i, :] for i in range(2)], "ob")
        psS = self.prot([(0, 512), (512, 1024), (1024, 1536)], "psS")
        pO = [self.preg(1536 + 512 * i, 2048 + 512 * i) for i in range(4)]
        pF = self.preg(3584, 4096)
        scale = 64 ** -0.5
        for lb in range(NB):
            tl = [lb] + list(range(2 + 16 * lb, 18 + 16 * lb))
            for hh in range(8):
                kT, kTb = ktr.next()
                qT, qTb = qtr.next()
                vh, vhb = vhr.next()
                for (dst, dbuf, row) in ((kT, kTb, 8 + hh), (qT, qTb, hh)):
                    kb.dma("sp", dst[:, 0:C], self.QK[row, :, ctx_tok(lb):ctx_tok(lb) + C],
                           reads=[self.QKb[t] for t in tl], writes=[dbuf])
                    kb.dma("sp", dst[:, C:C + S], self.QK[row, :, lat_tok(lb):lat_tok(lb) + S],
                           reads=[self.QKb[t] for t in tl], writes=[dbuf])
                kb.dma("sp", vh[:, 0:2, :],
                       self.VS[ctx_tok(lb):ctx_tok(lb) + C, hh * 128:(hh + 1) * 128].rearrange("(k p) e -> p k e", p=128),
                       reads=[self.VSb[t] for t in tl], writes=[vhb])
                for k8 in range(4):
                    r0 = lat_tok(lb) + k8 * 1024
                    kb.dma("sp", vh[:, 2 + 8 * k8:10 + 8 * k8, :],
                           self.VS[r0:r0 + 1024, hh * 128:(hh + 1) * 128].rearrange("(k p) e -> p k e", p=128),
                           reads=[self.VSb[t] for t in tl], writes=[vhb])
                for qt in range(9):
                    if qt == 0:
                        q0, nq, nkt = 0, C, 2
                    else:
                        q0, nq, nkt = C + (qt - 1) * 512, 512, NK
                    ntile = 2 * nkt
                    sl = {}

                    def emit_S(i):
                        kt, mp = i // 2, i % 2
                        pap, pb = psS.next()
                        lo = mp * 64
                        kb.mm(pap[:, 0:nq], pb, [(kT[lo:lo + 64, kt * 128:(kt + 1) * 128], qT[lo:lo + 64, q0:q0 + nq])],
                              reads=[kTb, qTb])
                        sl[i] = (pap, pb)

                    emit_S(0)
                    emit_S(1)
                    for i in range(ntile):
                        kt, mp = i // 2, i % 2
                        pap, pb = sl.pop(i)
                        pt, ptb = prot.next()
                        kb.op("act", lambda e, pt=pt, pap=pap: e.activation(out=pt[:, 0:nq], in_=pap[:, 0:nq], func=AF.Exp,
                                                                           scale=scale), reads=[pb], writes=[ptb])
                        if i + 2 < ntile:
                            emit_S(i + 2)
                        oap, obuf = pO[2 * mp]
                        lap, lbuf = pO[2 * mp + 1]
                        first, lastk = (kt == 0), (kt == nkt - 1)
                        kb.op("pe", lambda e, oap=oap, pt=pt, kt=kt, first=first, lastk=lastk: e.matmul(
                            oap[:, 0:nq], lhsT=vh[:, kt, :], rhs=pt[:, 0:nq], start=first, stop=lastk),
                            reads=[ptb, vhb], writes=[obuf], inc=False)
                        kb.op("pe", lambda e, lap=lap, pt=pt, first=first, lastk=lastk: e.matmul(
                            lap[:, 0:nq], lhsT=self.ones[:], rhs=pt[:, 0:nq], start=first, stop=lastk),
                            reads=[ptb, self.cb], writes=[lbuf], inc=True)
                    r1, r2, o1, o2, rs = [fin[:, i, 0:nq] for i in range(5)]
                    kb.op("dve", lambda e: e.reciprocal(out=r1, in_=pO[1][0][:, 0:nq]), reads=[pO[1][1]], writes=[finb[0]])
                    kb.op("dve", lambda e: e.reciprocal(out=r2, in_=pO[3][0][:, 0:nq]), reads=[pO[3][1]], writes=[finb[1]])
                    kb.op("pool", lambda e: e.tensor_scalar(out=r2, in0=r2, scalar1=nlam, scalar2=None, op0=ALU.mult),
                          reads=[finb[1], scb], writes=[finb[1]])
                    kb.op("dve", lambda e: e.tensor_tensor(out=o1, in0=pO[0][0][:, 0:nq], in1=r1, op=ALU.mult),
                          reads=[pO[0][1], finb[0]], writes=[finb[2]])
                    kb.op("dve", lambda e: e.tensor_tensor(out=o2, in0=pO[2][0][:, 0:nq], in1=r2, op=ALU.mult),
                          reads=[pO[2][1], finb[1]], writes=[finb[3]])
                    kb.op("pool", lambda e: e.tensor_tensor(out=o1, in0=o1, in1=o2, op=ALU.add),
                          reads=[finb[2], finb[3]], writes=[finb[2]])
                    kb.op("act", lambda e: e.activation(out=sqf[:, 0:nq], in_=o1, func=AF.Square),
                          reads=[finb[2]], writes=[sqfb])
                    kb.mm(pF[0][:, 0:nq], pF[1], [(self.ones[:], sqf[:, 0:nq])], reads=[sqfb, self.cb])
                    kb.op("act", lambda e: e.activation(out=rs, in_=pF[0][:, 0:nq], func=AF.Ln, bias=EPS, scale=1.0 / 128),
                          reads=[pF[1]], writes=[finb[4]])
                    kb.op("act", lambda e: e.activation(out=rs, in_=rs, func=AF.Exp, scale=-0.5),
                          reads=[finb[4]], writes=[finb[4]])
                    ob, obb = obr.next()
                    kb.op("dve", lambda e, ob=ob: e.scalar_tensor_tensor(out=ob[:, 0:nq], in0=o1, scalar=gsub, in1=rs,
                                                                         op0=ALU.mult, op1=ALU.mult),
                          reads=[finb[2], finb[4], scb], writes=[obb])
                    tok = (ctx_tok(lb) if qt == 0 else lat_tok(lb) + (qt - 1) * 512)
                    kb.dma("pool", self.MT[hh, :, tok:tok + nq], ob[:, 0:nq], reads=[obb],
                           writes=[self.MTb[t] for t in tl])
                    self.bg_step()
        kb.barrier()
        self.release(mk)

    def mla_front(self, l, wd_dram, mgq, mgkv, wuq_dram, wukv_dram):
        kb = self.kb
        mk = self.mark()
        c = self.alloc_common()
        wb = Buf("mlaw")
        wd = self.sb("wd", [128, 8, 416], BF16)
        self.load_w_bf16(wd[:], wd_dram.rearrange("(kc p) n -> p kc n", p=128), wb)
        wuq = self.sb("wuq", [128, 2, 1536], BF16)
        self.load_w_bf16(wuq[:], wuq_dram.rearrange("(kc p) n -> p kc n", p=128), wb)
        wuk = self.sb("wuk", [128, 16, 64], BF16)
        wuv = self.sb("wuv", [128, 16, 64], BF16)
        src = wukv_dram.rearrange("k (h t e) -> k h t e", t=2, e=64)
        self.load_w_bf16(wuk[:], src[:, :, 0, :], wb)
        self.load_w_bf16(wuv[:], src[:, :, 1, :], wb)
        gq = self.sb("gq", [128, 2], F32)
        gkv = self.sb("gkv", [128, 1], F32)
        kb.dma("sp", gq[:], mgq, writes=[wb])
        kb.dma("sp", gkv[:], mgkv, writes=[wb])
        permM = self.sb("permM", [96, 96], BF16)
        permK = self.sb("permK", [32, 32], BF16)
        kb.dma("sp", permM[:], self.cs["permM"], writes=[wb])
        kb.dma("sp", permK[:], self.cs["permK"], writes=[wb])
        csm_t = [self.sb("csM%d" % i, [96, 2, T], F32) for i in range(2)]
        csmr = Rot([x[:] for x in csm_t], "csM")
        csk_t = [self.sb("csK%d" % i, [32, 2, T], F32) for i in range(2)]
        cskr = Rot([x[:] for x in csk_t], "csK")
        cqn = self.sb("cqn", [128, 3, T], BF16)
        cqnb = Buf("cqn")
        rsq = self.sb("rsq", [128, 2, T], F32)
        rsqb = [Buf("rsq0"), Buf("rsq1")]
        kr_t = self.sb("kr", [32, 2, T], BF16)
        krb = [Buf("kr0"), Buf("kr1")]
        qs_t = [self.sb("qsb%d" % i, [96, 16, T], BF16) for i in range(2)]
        qsr = Rot([x[:] for x in qs_t], "qsb")
        qraw_t = self.sb("qrawm", [96, 3, T], BF16)
        qrr = Rot([qraw_t[:, i, :] for i in range(3)], "qrawm")
        t1_t = self.sb("t1m", [96, 3, T], F32)
        t1r = Rot([t1_t[:, i, :] for i in range(3)], "t1m")
        t2_t = self.sb("t2m", [96, 3, T], F32)
        t2r = Rot([t2_t[:, i, :] for i in range(3)], "t2m")
        kn_t = [self.sb("kn%d" % i, [128, 8, T], BF16) for i in range(2)]
        knr = Rot([x[:] for x in kn_t], "kn")
        va_t = [self.sb("va%d" % i, [128, 2, 16, 128], BF16) for i in range(2)]
        vab = [Buf("va0"), Buf("va1")]
        for i in range(2):
            kb.op("pool", lambda e, i=i: e.memset(va_t[i][:], 1.0), writes=[vab[i]])
        pstat = self.preg(0, T)
        pstat2 = self.preg(512, 512 + T)
        pd = self.preg(1024, 2048)
        pq = self.psum_slots(4, 2, T, "pqm")
        pr = self.psum_slots(6, 1, T, "prm")
        pv = self.prot([(3584, 4096)], "pvm")
        tiles = list(range(NT))
        for it, t in enumerate(tiles):
            ti = tinfo(t)
            tok0 = ti["tok0"]
            xt, xb = self.load_xt(c, l, t, False)
            hb, hbb = self.modulate(c, xt, xb, l, ti["cond"], "a", pstat)
            pdv = pd[0].rearrange("p (c t) -> p c t", t=T)
            cols = [(0, 128), (128, 128), (256, 128), (384, 32)]
            for ci, (c0, cw) in enumerate(cols):
                for k in range(8):
                    kb.op("pe", lambda e, ci=ci, c0=c0, cw=cw, k=k: e.matmul(
                        pdv[0:cw, ci, :], lhsT=wd[:, k, c0:c0 + cw], rhs=hb[:, k, :], start=(k == 0), stop=(k == 7)),
                        reads=[hbb, wb] if (ci == 0 and k == 0) else (), writes=[pd[1]] if (ci == 0 and k == 0) else (),
                        inc=(ci == 3 and k == 7))
            sq, sqb = c["sq"]
            kb.op("act", lambda e: e.activation(out=sq[:, 0:3, :], in_=pdv[:, 0:3, :], func=AF.Square),
                  reads=[pd[1]], writes=[sqb])
            kb.mm(pstat[0], pstat[1], [(self.ones[:], sq[:, 0, :]), (self.ones[:], sq[:, 1, :])], reads=[sqb, self.cb])
            kb.mm(pstat2[0], pstat2[1], [(self.ones[:], sq[:, 2, :])], reads=[sqb, self.cb])
            for i, (pst, n) in enumerate(((pstat, 256), (pstat2, 128))):
                kb.op("act", lambda e, i=i, pst=pst, n=n: e.activation(out=rsq[:, i, :], in_=pst[0], func=AF.Ln, bias=EPS,
                                                                     scale=1.0 / n), reads=[pst[1]], writes=[rsqb[i]])
                kb.op("act", lambda e, i=i: e.activation(out=rsq[:, i, :], in_=rsq[:, i, :], func=AF.Exp, scale=-0.5),
                      reads=[rsqb[i]], writes=[rsqb[i]])
            for ci in range(3):
                g = gq[:, ci:ci + 1] if ci < 2 else gkv[:, 0:1]
                ri = 0 if ci < 2 else 1
                kb.op("dve", lambda e, ci=ci, g=g, ri=ri: e.scalar_tensor_tensor(
                    out=cqn[:, ci, :], in0=pdv[:, ci, :], scalar=g, in1=rsq[:, ri, :], op0=ALU.mult, op1=ALU.mult),
                    reads=[pd[1], rsqb[ri], wb], writes=[cqnb])
            kb.op("act", lambda e: e.copy(out=kr_t[:, 0, :], in_=pdv[0:32, 3, :]), reads=[pd[1]], writes=[krb[0]])
            if ti["ctx"]:
                kr_fin, kr_finb = kr_t[:, 0, :], krb[0]
            else:
                csk, cskb = cskr.next()
                kb.dma("sp", csk[:, 0, :], self.cs["cosK"][:, ti["pos"]:ti["pos"] + T], writes=[cskb])
                kb.dma("sp", csk[:, 1, :], self.cs["sinK"][:, ti["pos"]:ti["pos"] + T], writes=[cskb])
                pap2, pb2 = pr.next()
                kb.mm(pap2[0:32, :], pb2, [(permK[:], kr_t[:, 0, :])], reads=[krb[0], wb])
                t1, t1b = t1r.next()
                t2, t2b = t2r.next()
                kb.op("pool", lambda e, t1=t1: e.tensor_tensor(out=t1[0:32, :], in0=kr_t[:, 0, :], in1=csk[:, 0, :], op=ALU.mult),
                      reads=[krb[0], cskb], writes=[t1b])
                kb.op("dve", lambda e, t2=t2, pap2=pap2: e.tensor_tensor(out=t2[0:32, :], in0=pap2[0:32, :], in1=csk[:, 1, :],
                                                                         op=ALU.mult), reads=[pb2, cskb], writes=[t2b])
                kb.op("pool", lambda e, t1=t1, t2=t2: e.tensor_tensor(out=kr_t[:, 1, :], in0=t1[0:32, :], in1=t2[0:32, :],
                                                                     op=ALU.add), reads=[t1b, t2b], writes=[krb[1]])
                kr_fin, kr_finb = kr_t[:, 1, :], krb[1]
            kb.dma("pool", self.KR[:, tok0:tok0 + T], kr_fin, reads=[kr_finb], writes=[self.KRb[t]])
            if not ti["ctx"]:
                csm, csmb = csmr.next()
                kb.dma("sp", csm[:, 0, :], self.cs["cosM"][:, ti["pos"]:ti["pos"] + T], writes=[csmb])
                kb.dma("sp", csm[:, 1, :], self.cs["sinM"][:, ti["pos"]:ti["pos"] + T], writes=[csmb])
                qs, qsb_ = qsr.next()
                for h in range(16):
                    pap, pb = pq.next()
                    kb.mm(pap[0:96, :], pb, [(wuq[:, cc, h * 96:(h + 1) * 96], cqn[:, cc, :]) for cc in range(2)],
                          reads=[cqnb, wb])
                    qr, qrb = qrr.next()
                    kb.op("act", lambda e, qr=qr, pap=pap: e.copy(out=qr, in_=pap[0:96, :]), reads=[pb], writes=[qrb])
                    pap2, pb2 = pr.next()
                    kb.mm(pap2[0:96, :], pb2, [(permM[:], qr)], reads=[qrb, wb])
                    t1, t1b = t1r.next()
                    t2, t2b = t2r.next()
                    kb.op("pool", lambda e, t1=t1, qr=qr: e.tensor_tensor(out=t1, in0=qr, in1=csm[:, 0, :], op=ALU.mult),
                          reads=[qrb, csmb], writes=[t1b])
                    kb.op("dve", lambda e, t2=t2, pap2=pap2: e.tensor_tensor(out=t2, in0=pap2[0:96, :], in1=csm[:, 1, :],
                                                                             op=ALU.mult), reads=[pb2, csmb], writes=[t2b])
                    kb.op("pool", lambda e, t1=t1, t2=t2, h=h: e.tensor_tensor(out=qs[:, h, :], in0=t1, in1=t2, op=ALU.add),
                          reads=[t1b, t2b], writes=[qsb_])
                for h8 in range(2):
                    kb.dma("pool", self.QM[8 * h8:8 * h8 + 8, :, tok0:tok0 + T].rearrange("h p t -> p h t"),
                           qs[:, 8 * h8:8 * h8 + 8, :], reads=[qsb_], writes=[self.QMb[t]])
            kn, knb = knr.next()
            for hp in range(8):
                pap, pb = pq.next()
                kb.mm(pap, pb, [(wuk[:, 2 * hp:2 * hp + 2, :], cqn[:, 2, :])], reads=[cqnb, wb])
                if hp % 2 == 0:
                    kb.op("act", lambda e, hp=hp, pap=pap: e.copy(out=kn[:, hp, :], in_=pap), reads=[pb], writes=[knb])
                else:
                    kb.op("dve", lambda e, hp=hp, pap=pap: e.tensor_copy(out=kn[:, hp, :], in_=pap), reads=[pb], writes=[knb])
            ktv = self.KT.rearrange("(hp two) r t -> two r hp t", two=2)
            for two in range(2):
                kb.dma("pool", ktv[two][:, :, tok0:tok0 + T], kn[two * 64:(two + 1) * 64, :, :], reads=[knb],
                       writes=[self.KTb[t]])
            va, vab_ = va_t[it % 2], vab[it % 2]
            for u in range(2):
                for nh in range(2):
                    pap, pb = pv.next()
                    kb.mm(pap, pb, [(cqn[:, 2, u * 128:(u + 1) * 128], wuv[:, nh * 8:(nh + 1) * 8, :])], reads=[cqnb, wb])
                    pvv = pap.rearrange("p (h e) -> p h e", e=64)
                    kb.op("dve", lambda e, u=u, nh=nh, pvv=pvv, va=va: e.tensor_copy(
                        out=va[:, u, nh * 8:(nh + 1) * 8:2, 0:64], in_=pvv[:, 0:8:2, :]), reads=[pb], writes=[vab_])
                    kb.op("act", lambda e, u=u, nh=nh, pvv=pvv, va=va: e.copy(
                        out=va[:, u, nh * 8 + 1:(nh + 1) * 8:2, 64:128], in_=pvv[:, 1:8:2, :]), reads=[pb], writes=[vab_])
            kb.dma("pool", self.VA[tok0:tok0 + T, :].rearrange("(u p) n -> p u n", p=128),
                   va[:].rearrange("p u h e -> p u (h e)"), reads=[vab_], writes=[self.VAb[t]])
        kb.barrier()
        self.release(mk)

    def mla_mid(self, l):
        kb = self.kb
        mk = self.mark()
        NK = 34
        kt_t = [self.sb("kTm%d" % i, [96, NK * 128], BF16) for i in range(2)]
        ktr = Rot([x[:] for x in kt_t], "kTm")
        qt_t = [self.sb("qTm%d" % i, [96, S], BF16) for i in range(2)]
        qtr = Rot([x[:] for x in qt_t], "qTm")
        vh_t = [self.sb("vhm%d" % i, [128, NK, 128], BF16) for i in range(2)]
        vhr = Rot([x[:] for x in vh_t], "vhm")
        p_t = self.sb("pTm", [128, 4, 512], BF16)
        prot = Rot([p_t[:, i, :] for i in range(4)], "pTm")
        rs_t = self.sb("rsm", [128, 2, 512], F32)
        rsr = Rot([rs_t[:, i, :] for i in range(2)], "rsm")
        op_t = [self.sb("opair%d" % i, [128, S], BF16) for i in range(1)]
        opr = Rot([x[:] for x in op_t], "opair")
        psS = self.prot([(512 * i, 512 * (i + 1)) for i in range(4)], "psSm")
        pOr = self.prot([(2048, 2560), (2560, 3072)], "pOm")
        scale = 96 ** -0.5
        for lb in range(NB):
            tl = [lb] + list(range(2 + 16 * lb, 18 + 16 * lb))
            for hp in range(8):
                opair, opb = opr.next()
                for par in range(2):
                    h = 2 * hp + par
                    kT, kTb = ktr.next()
                    qT, qTb = qtr.next()
                    vh, vhb = vhr.next()
                    kb.dma("sp", kT[0:64, 0:C], self.KT[h, :, ctx_tok(lb):ctx_tok(lb) + C],
                           reads=[self.KTb[t] for t in tl], writes=[kTb])
                    kb.dma("sp", kT[0:64, C:C + S], self.KT[h, :, lat_tok(lb):lat_tok(lb) + S],
                           reads=[self.KTb[t] for t in tl], writes=[kTb])
                    kb.dma("sp", kT[64:96, 0:C], self.KR[:, ctx_tok(lb):ctx_tok(lb) + C],
                           reads=[self.KRb[t] for t in tl], writes=[kTb])
                    kb.dma("sp", kT[64:96, C:C + S], self.KR[:, lat_tok(lb):lat_tok(lb) + S],
                           reads=[self.KRb[t] for t in tl], writes=[kTb])
                    kb.dma("sp", qT, self.QM[h, :, lat_tok(lb):lat_tok(lb) + S],
                           reads=[self.QMb[t] for t in tl[1:]], writes=[qTb])
                    kb.dma("sp", vh[:, 0:2, :],
                           self.VA[ctx_tok(lb):ctx_tok(lb) + C, h * 128:(h + 1) * 128].rearrange("(k p) e -> p k e", p=128),
                           reads=[self.VAb[t] for t in tl], writes=[vhb])
                    for k8 in range(4):
                        r0 = lat_tok(lb) + k8 * 1024
                        kb.dma("sp", vh[:, 2 + 8 * k8:10 + 8 * k8, :],
                               self.VA[r0:r0 + 1024, h * 128:(h + 1) * 128].rearrange("(k p) e -> p k e", p=128),
                               reads=[self.VAb[t] for t in tl], writes=[vhb])
                    for qt in range(8):
                        q0 = qt * 512
                        oap, obuf = pOr.next()
                        sl = {}

                        def emit_S(i):
                            pap, pb = psS.next()
                            kb.mm(pap, pb, [(kT[:, i * 128:(i + 1) * 128], qT[:, q0:q0 + 512])], reads=[kTb, qTb])
                            sl[i] = (pap, pb)

                        emit_S(0)
                        emit_S(1)
                        emit_S(2)
                        for i in range(NK):
                            pap, pb = sl.pop(i)
                            pt, ptb = prot.next()
                            kb.op("act", lambda e, pt=pt, pap=pap: e.activation(out=pt, in_=pap, func=AF.Exp, scale=scale),
                                  reads=[pb], writes=[ptb])
                            if i + 3 < NK:
                                emit_S(i + 3)
                            kb.op("pe", lambda e, pt=pt, i=i, oap=oap: e.matmul(
                                oap, lhsT=vh[:, i, :], rhs=pt, start=(i == 0), stop=(i == NK - 1)),
                                reads=[ptb, vhb], writes=[obuf], inc=True)
                        rs, rsb = rsr.next()
                        olo, ohi = (0, 64) if par == 0 else (64, 128)
                        slo, shi = (64, 128) if par == 0 else (0, 64)
                        kb.op("dve", lambda e, rs=rs, oap=oap: e.reciprocal(out=rs[olo:ohi, :], in_=oap[slo:shi, :]),
                              reads=[obuf], writes=[rsb])
                        kb.op("dve", lambda e, rs=rs, oap=oap: e.tensor_tensor(
                            out=opair[olo:ohi, q0:q0 + 512], in0=oap[olo:ohi, :], in1=rs[olo:ohi, :], op=ALU.mult),
                            reads=[obuf, rsb], writes=[opb])
                        self.bg_step()
                kb.dma("pool", self.MT[hp, :, lat_tok(lb):lat_tok(lb) + S], opair, reads=[opb],
                       writes=[self.MTb[t] for t in tl[1:]])
        kb.barrier()
        self.release(mk)


def _chunkT(v):
    v = np.asarray(v, np.float32)
    lead = v.shape[:-1]
    n = v.shape[-1] // 128
    a = v.reshape(lead + (n, 128))
    a = np.moveaxis(a, -1, 0)
    return np.ascontiguousarray(a)


def make_in_maps(inputs, consts, n_cores=8):
    f = lambda k: np.ascontiguousarray(np.asarray(inputs[k], np.float32))
    shared = {}
    for k in ("mod_w", "mlp_w1", "mlp_w2", "fourier_w", "diff_w_qkv", "diff_w_o", "mla_w_down", "mla_w_uq",
              "mla_w_ukv", "mla_w_o"):
        shared[k] = f(k)
    shared["mod_bT"] = _chunkT(f("mod_b"))
    shared["gmixT"] = _chunkT(f("norm_mix_g"))
    shared["gmlpT"] = _chunkT(f("norm_mlp_g"))
    shared["gfinT"] = _chunkT(f("final_g"))
    shared["fbT"] = _chunkT(f("fourier_b"))
    lam = np.stack([f("diff_lambda_q1")[0], f("diff_lambda_k1")[0], f("diff_lambda_q2")[0], f("diff_lambda_k2")[0]], 0)
    shared["dlam"] = np.ascontiguousarray(np.broadcast_to(lam[None], (128, 4, 64))).astype(np.float32)
    shared["dsub"] = np.ascontiguousarray(f("diff_subln_g")[0].reshape(128, 1))
    shared["mgq"] = _chunkT(f("mla_q_norm_g")[0])
    shared["mgkv"] = _chunkT(f("mla_kv_norm_g")[0])
    shared.update(consts)
    x = f("x")
    ctx = f("ctx")
    cvec = f("c")
    cc = f("c_ctx")
    maps = []
    for i in range(n_cores):
        m = dict(shared)
        m["x"] = np.ascontiguousarray(x[NB * i:NB * i + NB])
        m["ctx"] = np.ascontiguousarray(ctx[NB * i:NB * i + NB])
        cond = np.stack([cvec[NB * i], cvec[NB * i + 1], cc], 0)
        m["condT"] = np.ascontiguousarray(cond.reshape(3, 8, 128).transpose(2, 1, 0))
        maps.append(m)
    return maps


_CACHE = {}


def kernel(**inputs):
    if "consts" not in _CACHE:
        _CACHE["consts"] = make_consts()
    prog = Prog(n_layers=DEPTH, debug=False)
    nc = prog.build()
    maps = make_in_maps(inputs, _CACHE["consts"], 8)
    maps = [{k: v for k, v in m.items() if k in prog.din} for m in maps]
    res = run_bass_kernel_spmd(nc, maps, core_ids=list(range(8)))
    outs = [np.asarray(r["out"], np.float32).reshape(NB, S, D) for r in res.results]
    return np.concatenate(outs, axis=0)
```
